# Optimizing a Trainium2 kernel written in Bass

```python
import math
import jax, jax.numpy as jnp
from jax import lax
import numpy as np

D_MODEL = 1024
BATCH = 8
SEQ = 2048
DEPTH = 1

HEAD_DIM = 64
DIL_CONFIGS = ((128, 1), (512, 4), (2048, 16))
N_DIL_GROUPS = len(DIL_CONFIGS)
DIL_HEADS_PER_GROUP = 4
N_DIL_HEADS = N_DIL_GROUPS * DIL_HEADS_PER_GROUP
N_FOX_HEADS = 8
BLOCK = 128
ROPE_THETA = 500000.0
ROPE_DIM = HEAD_DIM // 4
D_FF = -(-(8 * D_MODEL) // (3 * 256)) * 256
EPS = 1e-6
NEG_INF = -1e30

DIL_WIDTH = N_DIL_HEADS * HEAD_DIM
DIL_OUT_WIDTH = DIL_HEADS_PER_GROUP * HEAD_DIM
FOX_WIDTH = N_FOX_HEADS * HEAD_DIM
IN_SPLITS = (DIL_WIDTH, DIL_WIDTH, DIL_WIDTH, FOX_WIDTH, FOX_WIDTH, FOX_WIDTH,
             N_FOX_HEADS, D_MODEL, D_MODEL)
IN_COLS = sum(IN_SPLITS)

kernel_name = "hybrid_dilated_fox_gated_block"


def rmsnorm(x, gain):
    xf = x.astype(jnp.float32)
    y = xf * lax.rsqrt(jnp.mean(xf * xf, axis=-1, keepdims=True) + EPS)
    return (y * gain.astype(jnp.float32)).astype(x.dtype)


def partial_rope(x, positions):
    half = ROPE_DIM // 2
    inv_freq = jnp.power(ROPE_THETA, -jnp.arange(0, ROPE_DIM, 2, dtype=jnp.float32) / ROPE_DIM)
    ang = positions[:, None] * inv_freq[None, :]
    cos = jnp.cos(ang)[None, :, None, :].astype(x.dtype)
    sin = jnp.sin(ang)[None, :, None, :].astype(x.dtype)
    x1, x2, rest = x[..., :half], x[..., half:ROPE_DIM], x[..., ROPE_DIM:]
    return jnp.concatenate([x1 * cos - x2 * sin, x2 * cos + x1 * sin, rest], axis=-1)


def dilated_window_attention(q, k, v, window, dilation):
    B, S, H, Dh = q.shape
    steps = window // dilation
    span = dilation * BLOCK
    s_pad = -(-S // span) * span
    L = s_pad // dilation
    nb = L // BLOCK

    def to_blocks(t):
        t = jnp.pad(t, ((0, 0), (0, s_pad - S), (0, 0), (0, 0)))
        t = t.reshape(B, L, dilation, H, Dh).transpose(0, 2, 1, 3, 4)
        return t.reshape(B * dilation, nb, BLOCK, H, Dh)

    qb, kb, vb = to_blocks(q), to_blocks(k), to_blocks(v)

    def with_prev(t):
        prev = jnp.pad(t[:, :-1], ((0, 0), (1, 0), (0, 0), (0, 0), (0, 0)))
        return jnp.concatenate([prev, t], axis=2)

    kc, vc = with_prev(kb), with_prev(vb)
    scale = 1.0 / math.sqrt(Dh)
    scores = jnp.einsum('znqhd,znkhd->znhqk', qb, kc).astype(jnp.float32) * scale
    qi = jnp.arange(BLOCK)[:, None]
    kj = jnp.arange(2 * BLOCK)[None, :]
    rel = qi + BLOCK - kj
    blk = jnp.arange(nb)[:, None, None]
    valid = (rel >= 0) & (rel <= steps) & (blk * BLOCK + kj - BLOCK >= 0)
    scores = jnp.where(valid[None, :, None, :, :], scores, NEG_INF)
    m = jnp.max(scores, axis=-1, keepdims=True)
    p = jnp.exp(scores - m)
    den = jnp.sum(p, axis=-1, keepdims=True)
    out = jnp.einsum('znhqk,znkhd->znqhd', (p / den).astype(v.dtype), vc)
    lse = (m + jnp.log(den))[..., 0].transpose(0, 1, 3, 2)

    def from_blocks(t):
        t = t.reshape((B, dilation, L) + t.shape[3:])
        t = jnp.moveaxis(t, 1, 2).reshape((B, s_pad) + t.shape[3:])
        return t[:, :S]

    return from_blocks(out), from_blocks(lse)


def forgetting_attention(q, k, v, log_f):
    B, S, H, Dh = q.shape
    F = jnp.cumsum(log_f.astype(jnp.float32), axis=1).transpose(0, 2, 1)
    scale = 1.0 / math.sqrt(Dh)
    outs = []
    for n in range(S // BLOCK):
        lo, hi = n * BLOCK, (n + 1) * BLOCK
        s = jnp.einsum('bqhd,bkhd->bhqk', q[:, lo:hi], k[:, :hi]).astype(jnp.float32) * scale
        s = s + (F[:, :, lo:hi, None] - F[:, :, None, :hi])
        causal = jnp.arange(lo, hi)[:, None] >= jnp.arange(hi)[None, :]
        s = jnp.where(causal[None, None], s, NEG_INF)
        p = jax.nn.softmax(s, axis=-1)
        outs.append(jnp.einsum('bhqk,bkhd->bqhd', p.astype(v.dtype), v[:, :hi]))
    return jnp.concatenate(outs, axis=1)


def token_mixer(h, w_in, w_proj_a, w_proj_b, w_out, b_forget):
    B, S, _ = h.shape
    proj = h @ w_in
    idx = list(np.cumsum(IN_SPLITS)[:-1])
    qa, ka, va, qb, kb, vb, f_logit, g_a, g_b = jnp.split(proj, idx, axis=-1)
    positions = jnp.arange(S, dtype=jnp.float32)

    qa = partial_rope(qa.reshape(B, S, N_DIL_HEADS, HEAD_DIM), positions)
    ka = partial_rope(ka.reshape(B, S, N_DIL_HEADS, HEAD_DIM), positions)
    va = va.reshape(B, S, N_DIL_HEADS, HEAD_DIM)
    group_out, group_lse = [], []
    for g, (window, dilation) in enumerate(DIL_CONFIGS):
        sl = slice(g * DIL_HEADS_PER_GROUP, (g + 1) * DIL_HEADS_PER_GROUP)
        o, lse = dilated_window_attention(qa[:, :, sl], ka[:, :, sl], va[:, :, sl], window, dilation)
        group_out.append(o)
        group_lse.append(lse)
    w_groups = jax.nn.softmax(jnp.stack(group_lse, axis=0), axis=0)
    out_a = jnp.sum(w_groups[..., None].astype(h.dtype) * jnp.stack(group_out, axis=0), axis=0)
    out_a = out_a.reshape(B, S, DIL_OUT_WIDTH)

    log_f = jax.nn.log_sigmoid((f_logit + b_forget).astype(jnp.float32))
    out_b = forgetting_attention(qb.reshape(B, S, N_FOX_HEADS, HEAD_DIM),
                                 kb.reshape(B, S, N_FOX_HEADS, HEAD_DIM),
                                 vb.reshape(B, S, N_FOX_HEADS, HEAD_DIM), log_f)
    out_b = out_b.reshape(B, S, FOX_WIDTH)

    merged = jax.nn.sigmoid(g_a) * (out_a @ w_proj_a) + jax.nn.sigmoid(g_b) * (out_b @ w_proj_b)
    return merged @ w_out


def swiglu(h, w_gate, w_up, w_down):
    return (jax.nn.silu(h @ w_gate) * (h @ w_up)) @ w_down


def setup_inputs(seed: int = 0) -> dict:
    key = jax.random.key(seed)
    ks = jax.random.split(key, 14)
    f32 = jnp.float32

    def dense(k, fan_in, fan_out):
        return jax.random.normal(k, (DEPTH, fan_in, fan_out), f32) * fan_in ** -0.5

    def gain(k):
        return 1.0 + 0.05 * jax.random.normal(k, (DEPTH, D_MODEL), f32)

    return {
        "x": jax.random.normal(ks[0], (BATCH, SEQ, D_MODEL), f32),
        "w_in": dense(ks[1], D_MODEL, IN_COLS),
        "w_proj_a": dense(ks[2], DIL_OUT_WIDTH, D_MODEL),
        "w_proj_b": dense(ks[3], FOX_WIDTH, D_MODEL),
        "w_out": dense(ks[4], D_MODEL, D_MODEL),
        "b_forget": jax.random.uniform(ks[5], (DEPTH, N_FOX_HEADS), f32, minval=1.0, maxval=5.0),
        "w_ffn_gate": dense(ks[6], D_MODEL, D_FF),
        "w_ffn_up": dense(ks[7], D_MODEL, D_FF),
        "w_ffn_down": dense(ks[8], D_FF, D_MODEL),
        "norm_mix_pre": gain(ks[9]),
        "norm_mix_post": gain(ks[10]),
        "norm_ffn_pre": gain(ks[11]),
        "norm_ffn_post": gain(ks[12]),
    }


def reference(x, w_in, w_proj_a, w_proj_b, w_out, b_forget, w_ffn_gate, w_ffn_up, w_ffn_down,
              norm_mix_pre, norm_mix_post, norm_ffn_pre, norm_ffn_post):
    for layer in range(DEPTH):
        h = rmsnorm(x, norm_mix_pre[layer])
        mix = token_mixer(h, w_in[layer], w_proj_a[layer], w_proj_b[layer], w_out[layer], b_forget[layer])
        x = x + rmsnorm(mix, norm_mix_post[layer])
        h = rmsnorm(x, norm_ffn_pre[layer])
        ff = swiglu(h, w_ffn_gate[layer], w_ffn_up[layer], w_ffn_down[layer])
        x = x + rmsnorm(ff, norm_ffn_post[layer])
    return x
```

```python
import contextlib
import os
import numpy as np
import ml_dtypes
import concourse.bass as bass
import concourse.mybir as mybir
from concourse.bass_utils import run_bass_kernel_spmd

F32 = mybir.dt.float32
BF16 = mybir.dt.bfloat16
AF = mybir.ActivationFunctionType
ALU = mybir.AluOpType

S = 2048
D = 1024
NT = 16
KC = 8
DFF = 2816
NJ = 22
INC = 5896
EPS = 1e-6
C_QA, C_KA, C_VA, C_QB, C_KB, C_VB, C_F, C_GA, C_GB = 0, 768, 1536, 2304, 2816, 3328, 3840, 3848, 4872
DIL = (1, 4, 16)
NEG = -30000.0
RECIP_FAST = os.environ.get('RECIP_EXACT') is None


class Slot:
    def __init__(self, nc, st, name):
        self.sem = st.enter_context(nc.semaphore(name))
        self.n = 0


class Eng:
    def __init__(self, nc, st, raw, name):
        self.raw = raw
        self.name = name
        self.sem = st.enter_context(nc.semaphore("sem_" + name))
        self.n = 0
        self.seen = {}

    def wait(self, deps):
        for tok in deps:
            if tok is None:
                continue
            if isinstance(tok, list):
                self.wait(tok)
                continue
            sem, val = tok
            k = id(sem)
            if self.seen.get(k, 0) < val:
                self.raw.wait_ge(sem, val)
                self.seen[k] = val

    def op(self, meth, *a, deps=(), sig=True, **kw):
        self.wait(deps)
        ins = getattr(self.raw, meth)(*a, **kw)
        if sig:
            self.n += 1
            ins.then_inc(self.sem, 1)
            return (self.sem, self.n)
        return None

    def dma(self, out, in_, slot, deps=()):
        self.wait(deps)
        self.raw.dma_start(out=out, in_=in_).then_inc(slot.sem, 16)
        slot.n += 16
        return (slot.sem, slot.n)


class Bank:
    def __init__(self, t):
        self.t = t
        self.free = []


def host_consts():
    ident = np.eye(128, dtype=np.float32)
    p = np.arange(128)[:, None]
    f = np.arange(128)[None, :]
    masku = np.where(f >= p, 0.0, NEG).astype(np.float32)
    maskl = np.where(p >= f, 0.0, NEG).astype(np.float32)
    mask2 = np.concatenate([maskl, masku], axis=1)
    perm = np.zeros((128, 128), np.float32)
    for m in range(128):
        dm = m % 64
        if dm < 8:
            perm[m + 8, m] = 1.0
        elif dm < 16:
            perm[m - 8, m] = 1.0
    half = 8
    inv_freq = np.power(np.float32(500000.0), -np.arange(0, 16, 2, dtype=np.float32) / np.float32(16)).astype(np.float32)
    pos = np.arange(S, dtype=np.float32)
    ang = (pos[:, None] * inv_freq[None, :]).astype(np.float32)
    cos = np.cos(ang.astype(np.float64)).astype(np.float32).T
    sin = np.sin(ang.astype(np.float64)).astype(np.float32).T
    cosf = np.ones((128, S), np.float32)
    sinf = np.zeros((128, S), np.float32)
    for m in range(128):
        dm = m % 64
        if dm < 8:
            cosf[m] = cos[dm]
            sinf[m] = -sin[dm]
        elif dm < 16:
            cosf[m] = cos[dm - 8]
            sinf[m] = sin[dm - 8]
    rows = np.concatenate([np.ones((3, S), np.float32), -np.ones((3, S), np.float32)], axis=0)
    return dict(c_ident=ident, c_mask2=mask2, c_perm=perm, c_cosf=cosf, c_sinf=sinf, c_rows=rows)


def build(dbg=False):
    nc = bass.Bass("TRN2", target_bir_lowering=False)

    def din(name, shape):
        return nc.dram_tensor(name, list(shape), F32, kind="ExternalInput").ap()

    x = din("x", [S, D])
    w_in = din("w_in", [D, INC])
    w_a = din("w_a", [256, D])
    w_b = din("w_b", [512, D])
    w_o = din("w_o", [D, D])
    b_f = din("b_f", [8, 1])
    w_g = din("w_g", [D, DFF])
    w_u = din("w_u", [D, DFF])
    w_d = din("w_d", [DFF, D])
    g_pre = din("g_pre", [1, D])
    g_post = din("g_post", [1, D])
    g_fpre = din("g_fpre", [1, D])
    g_fpost = din("g_fpost", [1, D])
    c_ident = din("c_ident", [128, 128])
    c_mask2 = din("c_mask2", [128, 256])
    c_perm = din("c_perm", [128, 128])
    c_cosf = din("c_cosf", [128, S])
    c_sinf = din("c_sinf", [128, S])
    c_rows = din("c_rows", [6, S])
    out = nc.dram_tensor("out", [S, D], F32, kind="ExternalOutput").ap()
    dbg_outs = {}

    def dout(name, shape):
        t = nc.dram_tensor(name, list(shape), F32, kind="ExternalOutput").ap()
        dbg_outs[name] = t
        return t

    w_in_v = w_in.rearrange("(kc p) n -> p kc n", p=128)
    w_g_v = w_g.rearrange("(kc p) n -> p kc n", p=128)
    w_u_v = w_u.rearrange("(kc p) n -> p kc n", p=128)
    w_o_v = w_o.rearrange("(kc p) n -> p kc n", p=128)
    w_a_v = w_a.rearrange("(kc p) n -> p kc n", p=128)
    w_b_v = w_b.rearrange("(kc p) n -> p kc n", p=128)

    with contextlib.ExitStack() as st:
        E = st.enter_context
        pe = Eng(nc, st, nc.tensor, "pe")
        act = Eng(nc, st, nc.scalar, "act")
        dve = Eng(nc, st, nc.vector, "dve")
        pool = Eng(nc, st, nc.gpsimd, "pool")
        sp = Eng(nc, st, nc.sync, "sp")
        engines = [pe, act, dve, pool, sp]
        slot_id = [0]

        def new_slot(stk=st):
            slot_id[0] += 1
            return Slot(nc, stk, f"ds{slot_id[0]}")

        def sb(stk, name, shape, dt):
            slot_id[0] += 1
            return stk.enter_context(nc.sbuf_tensor(f"{name}_{slot_id[0]}", list(shape), dt))

        def barrier(extra=()):
            toks = list(extra)
            for e in (pe, act, dve):
                toks.append(e.op("drain"))
            for e in engines:
                e.wait(toks)

        dbl = [E(nc.psum_tensor(f"ps{i}", [128, 1024], F32)) for i in range(4)]
        banks = [Bank(dbl[i // 2][:, (i % 2) * 512:(i % 2 + 1) * 512]) for i in range(8)]

        def bank_bf16(b):
            return b.t[:].bitcast(BF16)

        hT = sb(st, "hT", [128, KC, S], BF16)
        ident = sb(st, "ident", [128, 128], BF16)
        mask2 = sb(st, "mask2", [128, 256], BF16)
        perm = sb(st, "perm", [128, 128], BF16)
        onesrow = sb(st, "onesrow", [128, 128], F32)
        cslot = new_slot()
        ctoks = []
        ctoks.append(pool.dma(ident[:], c_ident, cslot))
        ctoks.append(pool.dma(mask2[:], c_mask2, cslot))
        ctoks.append(pool.dma(perm[:], c_perm, cslot))
        dve.op("memset", onesrow[:], 0.0, sig=False)
        tok_ones = dve.op("memset", onesrow[64:65, :], 1.0)
        const_tok = [ctoks[-1], tok_ones]

        def transposes_to(bank, src_tile, deps):
            pe.wait(bank.free)
            pe.wait(deps)
            pe.wait(const_tok)
            bv = bank_bf16(bank)
            tok = None
            for kc in range(KC):
                tok = pe.op("transpose", out=bv[:, kc * 128:(kc + 1) * 128], in_=src_tile[:, kc * 128:(kc + 1) * 128],
                            identity=ident[:], sig=(kc == KC - 1))
            return tok

        with contextlib.ExitStack() as sa:
            xall = sb(sa, "xall", [128, NT, D], F32)
            gpre = sb(sa, "gpre", [128, D], F32)
            ss = sb(sa, "ss", [128, NT], F32)
            lnv = sb(sa, "lnv", [128, NT], F32)
            rstd = sb(sa, "rstd", [128, NT], F32)
            junk = sb(sa, "junkA", [128, D], BF16)
            hb = [sb(sa, f"hb{i}", [128, D], BF16) for i in range(2)]
            gslot = new_slot()
            tok_g = sp.dma(gpre[:], g_pre.partition_broadcast(128), gslot)
            xslots = [new_slot() for _ in range(NT)]
            tokx = [sp.dma(xall[:, t, :], x[t * 128:(t + 1) * 128, :], xslots[t]) for t in range(NT)]
            hb_free = [[], []]
            pend_cp = None

            def emit_cp(t, bk, t_tr):
                dstv = hT[:, :, t * 128:(t + 1) * 128]
                srcv = bank_bf16(bk).rearrange("p (k n) -> p k n", k=KC)
                t_cp = dve.op("tensor_copy", out=dstv, in_=srcv, deps=[t_tr])
                bk.free = [t_cp]

            for t in range(NT):
                i = t % 2
                tsq = act.op("activation", out=junk[:], in_=xall[:, t, :], func=AF.Square, accum_out=ss[:, t:t + 1],
                             deps=[tokx[t]])
                t_ln = act.op("activation", out=lnv[:, t:t + 1], in_=ss[:, t:t + 1], func=AF.Ln, scale=1.0 / D, bias=EPS, deps=[tsq])
                t_rs = act.op("activation", out=rstd[:, t:t + 1], in_=lnv[:, t:t + 1], func=AF.Exp, scale=-0.5, deps=[t_ln])
                t_h = dve.op("scalar_tensor_tensor", out=hb[i][:], in0=xall[:, t, :], scalar=rstd[:, t:t + 1], in1=gpre[:],
                             op0=ALU.mult, op1=ALU.mult, deps=[t_rs, tok_g, hb_free[i]])
                bk = banks[t % 4]
                t_tr = transposes_to(bk, hb[i], [t_h])
                hb_free[i] = [t_tr]
                if pend_cp is not None:
                    emit_cp(*pend_cp)
                pend_cp = (t, bk, t_tr)
            emit_cp(*pend_cp)
            barrier()
        if dbg:
            d_hT = dout("d_hT", [128, KC * S])
            dslot = new_slot()
            with contextlib.ExitStack() as sd:
                hTf = sb(sd, "dbg_hTf", [128, KC * S], F32)
                t1 = dve.op("tensor_copy", out=hTf[:], in_=hT[:].rearrange("p k n -> p (k n)"))
                sp.dma(d_hT, hTf[:], dslot, deps=[t1])
                sp.wait([(dslot.sem, dslot.n)])
                barrier()

        S_banks = [banks[0], banks[1], banks[2]]
        O_banks = [banks[3], banks[4]]
        BCb = banks[5]
        PJ = [banks[6], banks[7]]

        def proj_fm(evac, wtile, col0, deps_w, pj=None):
            for c in range(4):
                bk = (pj or PJ)[c % 2]
                pe.wait(bk.free)
                tok = None
                for kc in range(KC):
                    tok = pe.op("matmul", bk.t[:], lhsT=wtile[:, kc, col0:col0 + 128], rhs=hT[:, kc, c * 512:(c + 1) * 512],
                                start=(kc == 0), stop=(kc == KC - 1), sig=(kc == KC - 1), deps=(deps_w if kc == 0 else ()))
                evac(c, bk, tok)

        with contextlib.ExitStack() as sbc:
            OBT = sb(sbc, "OBT", [128, 4, S], BF16)
            OAT = sb(sbc, "OAT", [128, 2, S], BF16)
            wring = [sb(sbc, f"wring{i}", [128, KC, 256], BF16) for i in range(4)]
            wf = sb(sbc, "wf", [128, KC, 8], BF16)
            fslot = new_slot()
            t_wf = pool.dma(wf[:], w_in_v[:, :, C_F:C_F + 8], fslot)
            WA = sb(sbc, "WA", [128, 2, D], BF16)
            WB = sb(sbc, "WB", [128, 4, D], BF16)
            ab_slot = new_slot()
            wr_free = [[] for _ in range(8)]
            wr_slot = [new_slot() for _ in range(8)]
            wr_ctr = [0]
            wr_rel = [True] * 8

            def ring_full(src_ap):
                if wr_ctr[0] % 2:
                    wr_ctr[0] += 1
                h0 = wr_ctr[0] % 8
                wr_ctr[0] += 2
                assert wr_rel[h0] and wr_rel[h0 + 1], "ring slot reused before its release was recorded"
                wr_rel[h0] = wr_rel[h0 + 1] = False
                tile_ = wring[h0 // 2]
                tok = pool.dma(tile_[:], src_ap, wr_slot[h0], deps=[wr_free[h0], wr_free[h0 + 1]])

                def release(tokens):
                    wr_free[h0] = tokens
                    wr_free[h0 + 1] = tokens
                    wr_rel[h0] = wr_rel[h0 + 1] = True
                return tile_, tok, release

            def ring_half(src_ap):
                h0 = wr_ctr[0] % 8
                wr_ctr[0] += 1
                assert wr_rel[h0], "ring half-slot reused before its release was recorded"
                wr_rel[h0] = False
                view = wring[h0 // 2][:, 4 * (h0 % 2):4 * (h0 % 2) + 4, :].rearrange("p a b -> p (a b)")
                tok = pool.dma(view, src_ap, wr_slot[h0], deps=[wr_free[h0]])

                def release(tokens):
                    wr_free[h0] = tokens
                    wr_rel[h0] = True
                return view, tok, release

            ring_pre = {}

            def pre_issue(key, src_ap):
                ring_pre[key] = ring_full(src_ap)

            def get_full(key, src_ap):
                if key in ring_pre:
                    return ring_pre.pop(key)
                return ring_full(src_ap)

            mvslot = new_slot()
            state = dict(rd_free=[], bcs_free=[], tmpB_free=[], n1_free=[])
            pending = []

            def tick():
                for p_ in pending:
                    p_[0] -= 1
                while pending and pending[0][0] <= 0:
                    pending.pop(0)[1]()

            def flush():
                while pending:
                    pending.pop(0)[1]()

            def normalize(src_ap_num, src_ap_den, dst, deps, on_done, res=None):
                if res is None:
                    res = state["res0"]
                rdt, bcb, bct = res["rd"], res["bank"], res["bcs"]
                if res.get("dve_recip"):
                    t_ex = dve.op("reciprocal", out=rdt[64:65, :], in_=src_ap_den, deps=[deps, res["rd_free"]])
                else:
                    t_ln = act.op("activation", out=rdt[64:65, :], in_=src_ap_den, func=AF.Ln, deps=[deps, res["rd_free"]])
                    t_ex = act.op("activation", out=rdt[64:65, :], in_=rdt[64:65, :], func=AF.Exp, scale=-1.0, deps=[t_ln])

                def later():
                    if res.get("take"):
                        _d, bA, bB = res["take"]()
                        pe.wait([bA.free, bB.free])
                        bcb_ = bA
                    else:
                        bA = bB = None
                        bcb_ = bcb
                        pe.wait(bcb_.free)
                    t_bc = pe.op("matmul", bcb_.t[:, :], lhsT=onesrow[:, :], rhs=rdt[:, :], start=True, stop=True,
                                 deps=[t_ex, tok_ones])
                    res["rd_free"] = [t_bc]
                    t_cp = dve.op("tensor_copy", out=bct[:, :], in_=bcb_.t[0:64, :], deps=[t_bc, res["bcs_free"]])
                    bcb_.free = [t_cp]
                    if bB is not None:
                        bB.free = [t_cp]
                    t_mul = dve.op("tensor_tensor", out=dst, in0=src_ap_num, in1=bct[:, :], op=ALU.mult, deps=[t_cp, deps])
                    res["bcs_free"] = [t_mul]
                    on_done(t_mul)
                pending.append([res.get("defer", 2), later])

            with contextlib.ExitStack() as sf:
                rd = sb(sf, "rd_f", [128, 512], F32)
                bcs = sb(sf, "bcs_f", [64, 512], F32)
                tmpB = sb(sf, "tmpB_f", [64, S], BF16)
                spl = [sb(sf, f"spl{i}", [8, S], BF16) for i in range(3)]
                if True:
                    bft = sb(sf, "bft", [8, 1], F32)
                    nbf = sb(sf, "nbf", [8, 1], F32)
                    ev = sb(sf, "ev", [8, S], F32)
                    ones8 = sb(sf, "ones8", [8, S], BF16)
                    G = sb(sf, "G", [8, S], F32)
                    fslot2 = new_slot()
                    t_bf = sp.dma(bft[:], b_f, fslot2)
                    t_nb = dve.op("tensor_scalar", out=nbf[:], in0=bft[:], scalar1=-1.0, scalar2=None, op0=ALU.mult, deps=[t_bf])
                    t_e = None
                    for c in range(4):
                        bk = PJ[c % 2]
                        pe.wait(bk.free)
                        tok = None
                        for kc in range(KC):
                            tok = pe.op("matmul", bk.t[0:8, :], lhsT=wf[:, kc, :], rhs=hT[:, kc, c * 512:(c + 1) * 512],
                                        start=(kc == 0), stop=(kc == KC - 1), sig=(kc == KC - 1), deps=[t_wf])
                        t_e = act.op("activation", out=ev[:, c * 512:(c + 1) * 512], in_=bk.t[0:8, :], func=AF.Exp, scale=-1.0,
                                     bias=nbf[:, 0:1], deps=[tok, t_nb])
                        bk.free = [t_e]
                    t_sp = act.op("activation", out=ev[:], in_=ev[:], func=AF.Ln, scale=1.0, bias=1.0, deps=[t_e])
                    t_o = dve.op("memset", ones8[:], 1.0)
                    t_G = dve.op("tensor_tensor_scan", out=G[:], data0=ones8[:], data1=ev[:], initial=0.0, op0=ALU.mult,
                                 op1=ALU.add, deps=[t_sp, t_o])
                    t_G8 = dve.op("tensor_scalar", out=G[:], in0=G[:], scalar1=8.0, scalar2=None, op0=ALU.mult, deps=[t_G])
                    t_hi = dve.op("tensor_copy", out=spl[0][:], in_=G[:], deps=[t_G8])
                    t_r1 = dve.op("tensor_tensor", out=ev[:], in0=G[:], in1=spl[0][:], op=ALU.subtract, deps=[t_hi])
                    t_mid = dve.op("tensor_copy", out=spl[1][:], in_=ev[:], deps=[t_r1])
                    t_r2 = dve.op("tensor_tensor", out=G[:], in0=ev[:], in1=spl[1][:], op=ALU.subtract, deps=[t_mid])
                    t_lo = dve.op("tensor_copy", out=spl[2][:], in_=G[:], deps=[t_r2])
                    if dbg:
                        d_G = dout("d_G", [8, S])
                        t_a = dve.op("tensor_tensor", out=ev[:], in0=spl[0][:], in1=spl[1][:], op=ALU.add, deps=[t_lo])
                        t_b = dve.op("tensor_tensor", out=ev[:], in0=ev[:], in1=spl[2][:], op=ALU.add, deps=[t_a])
                        gsl = new_slot()
                        tkg = sp.dma(d_G, ev[:], gsl, deps=[t_b])
                        barrier([tkg])

                wqb = sb(sf, "wqb", [128, KC, 512], BF16)
                wkb = sb(sf, "wkb", [128, KC, 512], BF16)
                wvb = sb(sf, "wvb", [128, KC, 512], BF16)
                wslots = [new_slot() for _ in range(3)]
                t_wv = pool.dma(wvb[:], w_in_v[:, :, C_VB:C_VB + 512], wslots[2])
                t_wq = pool.dma(wqb[:], w_in_v[:, :, C_QB:C_QB + 512], wslots[0])
                t_wk = pool.dma(wkb[:], w_in_v[:, :, C_KB:C_KB + 512], wslots[1])
                V_all = sb(sf, "V_all", [128, NT, 8, 66], BF16)
                Qh = [sb(sf, f"Qh{i}", [128, S], BF16) for i in range(4)]
                Kh = [sb(sf, f"Kh{i}", [128, S], BF16) for i in range(4)]
                Pb = [sb(sf, f"Pf{i}", [128, 1024], BF16) for i in range(4)]
                P_free = [[] for _ in range(4)]
                sring = [(dbl[k_], banks[2 * k_], banks[2 * k_ + 1]) for k_ in range(3)]
                ring_i = [0]

                def take_slot():
                    sl = sring[ring_i[0] % 3]
                    ring_i[0] += 1
                    return sl
                hq_slots = [new_slot() for _ in range(4)]
                t_vm = pool.op("memset", V_all[:].rearrange("p t h d -> p (t h d)"), 1.0)
                t_rd0 = pool.op("memset", rd[:], 0.0)
                state["res0"] = dict(rd=rd, bank=None, take=take_slot, bcs=bcs, rd_free=[t_rd0], bcs_free=[], defer=3)
                h_init = []
                for i in range(4):
                    tq0 = pool.op("memset", Qh[i][:], 0.0)
                    tk0 = pool.op("memset", Kh[i][:], 0.0)
                    base = 64 if i % 2 == 0 else 0
                    pool.dma(Qh[i][base + 3:base + 6, :], c_rows[0:3, :], hq_slots[i], deps=[tq0])
                    h_init.append(pool.dma(Kh[i][base:base + 3, :], c_rows[3:6, :], hq_slots[i], deps=[tk0]))
                hfree = [[] for _ in range(4)]

                def prep_rows(h):
                    i = h % 4
                    base = 64 if i % 2 == 0 else 0
                    sp.wait([hfree[i], t_lo, h_init[i]])
                    tok = None
                    for r in range(3):
                        sp.dma(Qh[i][base + r:base + r + 1, :], spl[r][h:h + 1, :], hq_slots[i])
                        tok = sp.dma(Kh[i][base + 3 + r:base + 4 + r, :], spl[r][h:h + 1, :], hq_slots[i])
                    return tok

                for t in range(NT):
                    bk = PJ[t % 2]
                    pe.wait(bk.free)
                    tok = None
                    for kc in range(KC):
                        tok = pe.op("matmul", bk.t[:], lhsT=hT[:, kc, t * 128:(t + 1) * 128], rhs=wvb[:, kc, :],
                                    start=(kc == 0), stop=(kc == KC - 1), sig=(kc == KC - 1), deps=[t_wv])
                    src = bk.t[:].rearrange("p (h d) -> p h d", h=8)
                    t_ev = act.op("activation", out=V_all[:, t, :, 0:64], in_=src, func=AF.Copy, deps=[tok, t_vm])
                    bk.free = [t_ev]
                t_vall = [t_ev]
                v_done = []

                qk_ready = {}
                kaqa_tok = {}

                def proj_pair(hp):
                    iA = (2 * hp) % 4
                    iB = iA + 1
                    toks = []

                    def mk(dA, dB):
                        def evac(c, bk, tok):
                            cs = slice(c * 512, (c + 1) * 512)
                            dve.op("tensor_copy", out=dA[0:64, cs], in_=bk.t[0:64, :],
                                   deps=[tok, hfree[iA], hfree[iB], h_init[iA], h_init[iB]], sig=False)
                            t2 = dve.op("tensor_copy", out=dB[64:128, cs], in_=bk.t[64:128, :])
                            bk.free = [t2]
                            toks.append(t2)
                        return evac
                    for (ev_, wt_, tw_) in ((mk(Qh[iA], Qh[iB]), wqb, t_wq), (mk(Kh[iA], Kh[iB]), wkb, t_wk)):
                        for c in range(4):
                            _d, bA, bB = take_slot()
                            pe.wait([bA.free, bB.free])
                            tok = None
                            for kc in range(KC):
                                tok = pe.op("matmul", bA.t[:], lhsT=wt_[:, kc, hp * 128:(hp + 1) * 128], rhs=hT[:, kc, c * 512:(c + 1) * 512],
                                            start=(kc == 0), stop=(kc == KC - 1), sig=(kc == KC - 1), deps=([tw_] if kc == 0 else ()))
                            ev_(c, bA, tok)
                            bB.free = bA.free
                    qk_ready[hp] = toks
                    for hh in range(2):
                        kaqa_tok[2 * hp + hh] = prep_rows(2 * hp + hh)

                units = []
                for hp in range(4):
                    for hh in range(2):
                        for c in range(4):
                            nj = 4 * c + 4
                            for j0 in range(0, nj, 2):
                                its = [dict(j=j, last=(j == nj - 1), q0=max(j * 128, c * 512) - c * 512, diag=(j >= 4 * c))
                                       for j in (j0, j0 + 1)]
                                units.append(dict(hp=hp, hh=hh, h=2 * hp + hh, c=c, its=its, j0=j0))
                fO = [banks[6], banks[7]]
                chunk_ctr = [0]
                headB_toks = {}
                cur = {}

                def stage1(u, ui):
                    hp, hh, h, c = u["hp"], u["hh"], u["h"], u["c"]
                    if c == 0 and u["j0"] == 0 and hh == 0 and hp not in qk_ready:
                        proj_pair(hp)
                    if hh == 1 and c == 3 and u["j0"] == 0 and hp + 1 < 4:
                        proj_pair(hp + 1)
                    if hp == 3 and hh == 1 and c == 2 and u["j0"] == 0:
                        pre_issue(("D", 0, "q"), w_in_v[:, :, C_QA:C_QA + 256])
                        pre_issue(("D", 0, "k"), w_in_v[:, :, C_KA:C_KA + 256])
                        pre_issue(("D", 0, "v"), w_in_v[:, :, C_VA:C_VA + 256])
                    sd_, bA_, bB_ = take_slot()
                    u["dbl"] = sd_
                    u["bks"] = [bA_, bB_]
                    pe.wait([bA_.free, bB_.free])
                    tok = None
                    for k, it in enumerate(u["its"]):
                        q0, j = it["q0"], it["j"]
                        c0 = q0 if k == 0 else 512
                        it["c0"] = c0
                        qs = slice(c * 512 + q0, (c + 1) * 512)
                        ks = slice(j * 128, (j + 1) * 128)
                        tok = pe.op("matmul", u["dbl"][:, c0:c0 + 512 - q0], lhsT=Kh[h % 4][:, ks], rhs=Qh[h % 4][:, qs], start=True,
                                    stop=(not it["diag"]), sig=(k == 1 and not it["diag"]), deps=[qk_ready[hp], kaqa_tok[h]])
                        if it["diag"]:
                            tok = pe.op("matmul", u["dbl"][:, c0:c0 + 128], lhsT=ident[:], rhs=mask2[:, 128:256], start=False,
                                        stop=True, sig=(k == 1), deps=[const_tok])
                    u["s_tok"] = tok
                    hfree[h % 4] = [tok]
                    tick()

                def stage2(u, ui):
                    lo = u["its"][0]["q0"]
                    hi = 1024 - u["its"][1]["q0"]
                    u["p"] = Pb[ui % 4]
                    tok = act.op("activation", out=u["p"][:, lo:hi], in_=u["dbl"][:, lo:hi], func=AF.Exp, scale=0.125,
                                 deps=[u["s_tok"], P_free[ui % 4]])
                    u["bks"][0].free = [tok]
                    u["bks"][1].free = [tok]
                    u["p_tok"] = tok

                def stage3(u, ui):
                    hp, hh, h, c = u["hp"], u["hh"], u["h"], u["c"]
                    if u["j0"] == 0:
                        cur["obk"] = fO[chunk_ctr[0] % 2]
                        chunk_ctr[0] += 1
                        pe.wait(cur["obk"].free)
                    obk = cur["obk"]
                    tok = None
                    for k, it in enumerate(u["its"]):
                        q0, j, c0 = it["q0"], it["j"], it["c0"]
                        tok = pe.op("matmul", obk.t[0:65, q0:512], lhsT=V_all[:, j, h, 0:65], rhs=u["p"][:, c0:c0 + 512 - q0], start=(j == 0),
                                    stop=it["last"], sig=(k == 1), deps=[u["p_tok"], t_vall])
                    P_free[ui % 4] = [tok]
                    if u["its"][1]["last"]:
                        if hh == 0:
                            dst = OBT[0:64, hp, c * 512:(c + 1) * 512]
                        else:
                            dst = tmpB[0:64, c * 512:(c + 1) * 512]

                        def done(t_mul, obk=obk, hp=hp, hh=hh, c=c, h=h):
                            obk.free = [t_mul]
                            if hh == 1:
                                headB_toks.setdefault(hp, []).append(t_mul)
                                if c == 3:
                                    tk = sp.dma(OBT[64:128, hp, :], tmpB[0:64, :], mvslot, deps=[headB_toks[hp]])
                                    state["tmpB_free"] = [tk]
                                    v_done.append(tk)
                        extra = state["tmpB_free"] if hh == 1 else []
                        normalize(obk.t[0:64, :], obk.t[64:65, :], dst, [tok, extra], done)

                LAG = 2
                for ui, u in enumerate(units):
                    stage1(u, ui)
                    stage2(u, ui)
                    if ui >= LAG:
                        stage3(units[ui - LAG], ui - LAG)
                for ui in range(len(units) - LAG, len(units)):
                    stage3(units[ui], ui)
                flush()
                barrier([(mvslot.sem, mvslot.n)])

            if dbg:
                d_obt = dout("d_obt", [128, 4 * S])
                with contextlib.ExitStack() as sd:
                    obf = sb(sd, "dbg_obf", [128, 4 * S], F32)
                    t1 = dve.op("tensor_copy", out=obf[:], in_=OBT[:].rearrange("p k n -> p (k n)"))
                    dsl = new_slot()
                    sp.dma(d_obt, obf[:], dsl, deps=[t1])
                    sp.wait([(dsl.sem, dsl.n)])
                    barrier()
            with contextlib.ExitStack() as sdl:
                bcs = sb(sdl, "bcs_d", [64, 512], F32)
                tmpB = sb(sdl, "tmpB_d", [64, S], BF16)
                bcs2 = sb(sdl, "bcs_d2", [64, 512], F32)
                state.update(tmpB_free=[])
                cosf = sb(sdl, "cosf", [128, S], F32)
                sinf = sb(sdl, "sinf", [128, S], F32)
                cs_slot = new_slot()
                ACC = sb(sdl, "ACC", [65, 4, S], F32)
                QTd = [sb(sdl, f"QTd{i}", [128, S], BF16) for i in range(2)]
                Kd = [[sb(sdl, f"Kd{i}_{j}", [128, S], BF16) for j in range(2)] for i in range(2)]
                KTp = [sb(sdl, f"KTp{i}", [128, S], BF16) for i in range(2)]
                kd_slots = [[new_slot() for _ in range(2)] for _ in range(2)]
                ktp_free = [[], []]

                Vg = sb(sdl, "Vg", [128, 16, 4, 66], BF16)
                qbt = [sb(sdl, f"qbt{i}", [128, 512], BF16) for i in range(2)]
                t2b = [sb(sdl, f"t2b{i}", [128, 512], F32) for i in range(2)]
                Pd = [sb(sdl, f"Pd{i}", [128, 512], BF16) for i in range(4)]
                Pd_free = [[] for _ in range(4)]
                dil_ms = {}
                Bbanks = [banks[5], banks[2]]
                rp = dict(i=0, qbt_free=[[], []], t1_free=[[], []], t2_free=[[], []])
                dil_state = dict(qk_free=[], v_free=[], octr=0, sidx=0)

                def rope_proj(dsts, wtile, col0, deps_w, dst_free):
                    done_toks = []

                    def rest(c, bk, tokA, t_qb, i):
                        Bb = Bbanks[i % 2]
                        pe.wait(Bb.free)
                        t_B = pe.op("matmul", Bb.t[:], lhsT=perm[:], rhs=qbt[i % 2][:], start=True, stop=True,
                                    deps=[t_qb, const_tok])
                        rp["qbt_free"][i % 2] = [t_B]
                        cs = slice(c * 512, (c + 1) * 512)
                        t_1 = dve.op("tensor_tensor", out=bk.t[:], in0=bk.t[:], in1=cosf[:, cs], op=ALU.mult,
                                     deps=[tokA, t_qb, t_cs])
                        t_2 = dve.op("tensor_tensor", out=t2b[i % 2][:], in0=Bb.t[:], in1=sinf[:, cs], op=ALU.mult,
                                     deps=[t_B, rp["t2_free"][i % 2]])
                        Bb.free = [t_2]
                        t_3 = dve.op("tensor_tensor", out=dsts[:, cs], in0=bk.t[:], in1=t2b[i % 2][:], op=ALU.add,
                                     deps=[t_1, t_2, dst_free])
                        bk.free = [t_3]
                        rp["t2_free"][i % 2] = [t_3]
                        done_toks.append(t_3)

                    prev = None
                    for c in range(4):
                        bk = [banks[6], banks[7], banks[0], banks[1]][rp["i"] % 4]
                        pe.wait(bk.free)
                        tok = None
                        for kc in range(KC):
                            tok = pe.op("matmul", bk.t[:], lhsT=wtile[:, kc, col0:col0 + 128], rhs=hT[:, kc, c * 512:(c + 1) * 512],
                                        start=(kc == 0), stop=(kc == KC - 1), sig=(kc == KC - 1), deps=(deps_w if kc == 0 else ()))
                        i = rp["i"]
                        rp["i"] += 1
                        t_qb = act.op("activation", out=qbt[i % 2][:], in_=bk.t[:], func=AF.Copy, deps=[tok, rp["qbt_free"][i % 2]])
                        if prev is not None:
                            rest(*prev)
                        prev = (c, bk, tok, t_qb, i)
                    rest(*prev)
                    return done_toks, tok

                for g in range(int(os.environ.get('DBG_NG', '3'))):
                    d = DIL[g]
                    nb = 16 // d
                    wi = g % 2
                    wq_t, t_wq, rel_q = get_full(("D", g, "q"), w_in_v[:, :, C_QA + g * 256:C_QA + (g + 1) * 256])
                    if g == 0:
                        pool.dma(cosf[:], c_cosf, cs_slot)
                        t_cs = pool.dma(sinf[:], c_sinf, cs_slot)
                    wk_t, t_wk, rel_k = get_full(("D", g, "k"), w_in_v[:, :, C_KA + g * 256:C_KA + (g + 1) * 256])
                    wv_t, t_wv, rel_v = get_full(("D", g, "v"), w_in_v[:, :, C_VA + g * 256:C_VA + (g + 1) * 256])
                    if g == 0:
                        t_kd0 = [pool.op("memset", Kd[i][j][:], 0.0) for i in range(2) for j in range(2)]
                        t_vgm = pool.op("memset", Vg[:].rearrange("p t h d -> p (t h d)"), 1.0)
                    qk_ready = []
                    for pr in range(2):
                        dq, lastq = rope_proj(QTd[pr], wq_t, pr * 128, [t_wq], dil_state["qk_free"])
                        dk, lastk = rope_proj(KTp[pr], wk_t, pr * 128, [t_wk], ktp_free[pr])
                        tkA = sp.dma(Kd[pr][0][0:64, :], KTp[pr][0:64, :], kd_slots[pr][0], deps=[dk, dil_state["qk_free"], t_kd0])
                        tkB = sp.dma(Kd[pr][1][64:128, :], KTp[pr][64:128, :], kd_slots[pr][1], deps=[dk, dil_state["qk_free"], t_kd0])
                        ktp_free[pr] = [tkA, tkB]
                        qk_ready += dq + [tkA, tkB]
                    rel_q([lastq])
                    rel_k([lastk])
                    t_ev = None
                    for b in range(16):
                        r, n = divmod(b, nb)
                        bk = PJ[b % 2]
                        pe.wait(bk.free)
                        tok = None
                        for kc in range(KC):
                            lv = hT[:, kc, :].rearrange("p (i r) -> p r i", r=d)[:, r, 128 * n:128 * (n + 1)]
                            tok = pe.op("matmul", bk.t[:, 0:256], lhsT=lv, rhs=wv_t[:, kc, :], start=(kc == 0), stop=(kc == KC - 1),
                                        sig=(kc == KC - 1), deps=[t_wv])
                        src = bk.t[:, 0:256].rearrange("p (h d) -> p h d", h=4)
                        if b % 2 == 0:
                            t_ev = act.op("activation", out=Vg[:, b, :, 0:64], in_=src, func=AF.Copy, deps=[tok, t_vgm, dil_state["v_free"]])
                        else:
                            t_ev = dve.op("tensor_copy", out=Vg[:, b, :, 0:64], in_=src, deps=[tok, t_vgm, dil_state["v_free"]])
                        bk.free = [t_ev]
                    rel_v([tok])
                    if g == 2:
                        pool.dma(WA[:], w_a_v, ab_slot)
                        ab_tok = [pool.dma(WB[:], w_b_v, ab_slot)]
                        for ocg_ in range(2):
                            pre_issue(("C", 0, ocg_, "a"), w_in_v[:, :, C_GA + ocg_ * 256:C_GA + (ocg_ + 1) * 256])
                            pre_issue(("C", 0, ocg_, "b"), w_in_v[:, :, C_GB + ocg_ * 256:C_GB + (ocg_ + 1) * 256])
                    v_ready = [t_ev]
                    if True:
                        v_ready = []
                        v_ready = [(act.sem, act.n), (dve.sem, dve.n)]
                    ditems = [dict(hh=hh, bs=(b0, b0 + 1)) for hh in range(4) for b0 in range(0, 16, 2)]
                    if os.environ.get('DBG_NOATT'):
                        ditems = []
                        dil_state['acc_tok'] = []

                    def dstage1(u):
                        hh = u["hh"]
                        pr = hh // 2
                        Qv = QTd[pr][:, :].rearrange("p (i r) -> p r i", r=d)
                        Kv = Kd[pr][hh % 2][:, :].rearrange("p (i r) -> p r i", r=d)
                        i = dil_state["sidx"]
                        dil_state["sidx"] += 1
                        u["i"] = i
                        sbk = S_banks[i % 3]
                        u["sbk"] = sbk
                        pe.wait(sbk.free)
                        pe.wait(qk_ready)
                        pe.wait(const_tok)
                        tok = None
                        for k, b in enumerate(u["bs"]):
                            r, n = divmod(b, nb)
                            off = 256 * k
                            qsl = Qv[:, r, 128 * n:128 * (n + 1)]
                            if n >= 1:
                                pe.op("matmul", sbk.t[:, off:off + 128], lhsT=Kv[:, r, 128 * (n - 1):128 * n], rhs=qsl, start=True, stop=False,
                                      sig=False)
                            pe.op("matmul", sbk.t[:, off + 128:off + 256], lhsT=Kv[:, r, 128 * n:128 * (n + 1)], rhs=qsl, start=(n == 0),
                                  stop=False, sig=False)
                            if n >= 1:
                                tok = pe.op("matmul", sbk.t[:, off:off + 256], lhsT=ident[:], rhs=mask2[:, 0:256], start=False, stop=True,
                                            sig=(k == 1))
                            else:
                                tok = pe.op("matmul", sbk.t[:, off + 128:off + 256], lhsT=ident[:], rhs=mask2[:, 128:256], start=False,
                                            stop=True, sig=(k == 1))
                        u["s_tok"] = tok
                        dil_state["qk_free"] = [tok]

                    def dstage2(u):
                        r0, n0 = divmod(u["bs"][0], nb)
                        lo = 0 if n0 >= 1 else 128
                        i = u["i"]
                        tok = act.op("activation", out=Pd[i % 4][:, lo:512], in_=u["sbk"].t[:, lo:512], func=AF.Exp, scale=0.125,
                                     deps=[u["s_tok"], Pd_free[i % 4]])
                        u["sbk"].free = [tok]
                        u["p_tok"] = tok

                    def dstage3(u):
                        for k_, b_ in enumerate(u["bs"]):
                            dstage3_item(u, k_, b_)

                    def dstage3_item(it, k_, b):
                        hh = it["hh"]
                        poff = 256 * k_
                        r, n = divmod(b, nb)
                        i = it["i"]
                        col = (b % 4) * 128
                        if b % 4 == 0:
                            dil_state["obk"] = O_banks[dil_state["octr"] % 2]
                            dil_state["octr"] += 1
                            pe.wait(dil_state["obk"].free)
                        obk = dil_state["obk"]
                        pe.wait([it["p_tok"], v_ready])
                        if n >= 1:
                            pe.op("matmul", obk.t[0:65, col:col + 128], lhsT=Vg[:, b - 1, hh, 0:65], rhs=Pd[i % 4][:, poff:poff + 128], start=True,
                                  stop=False, sig=False)
                        tok = pe.op("matmul", obk.t[0:65, col:col + 128], lhsT=Vg[:, b, hh, 0:65], rhs=Pd[i % 4][:, poff + 128:poff + 256],
                                    start=(n == 0), stop=True)
                        Pd_free[i % 4] = [tok]
                        dil_state["v_free"] = [tok]
                        if b % 4 == 3:
                            q = b // 4
                            if d == 1:
                                dstv = ACC[0:65, hh, 512 * q:512 * (q + 1)]
                                src = obk.t[0:65, :]
                            elif d == 4:
                                dstv = ACC[0:65, hh, :].rearrange("p (i r) -> p r i", r=4)[:, q, :]
                                src = obk.t[0:65, :]
                            else:
                                dstv = ACC[0:65, hh, :].rearrange("p (f r) -> p r f", r=16)[:, 4 * q:4 * q + 4, :]
                                src = obk.t[0:65, :].rearrange("p (r f) -> p r f", r=4)
                            if g == 0:
                                if q % 2 == 0:
                                    t_a = act.op("activation", out=dstv, in_=src, func=AF.Copy, deps=[tok])
                                else:
                                    t_a = dve.op("tensor_copy", out=dstv, in_=src, deps=[tok])
                            else:
                                t_a = dve.op("tensor_tensor", out=dstv, in0=src, in1=dstv, op=ALU.add, deps=[tok, it.get("acc_dep", [])])
                            obk.free = [t_a]
                            dil_state["acc_tok"] = [(act.sem, act.n), (dve.sem, dve.n)]

                    LAGD = 2
                    for i, it in enumerate(ditems):
                        if g > 0:
                            it["acc_dep"] = dil_state.get("acc_prev", [])
                        dstage1(it)
                        dstage2(it)
                        if i >= LAGD:
                            dstage3(ditems[i - LAGD])
                    for i in range(max(0, len(ditems) - LAGD), len(ditems)):
                        dstage3(ditems[i])
                    dil_state["acc_prev"] = dil_state.get("acc_tok", [])

                acc_done = dil_state["acc_tok"]
                zt = [t2b[0], t2b[1]]
                t_z = [dve.op("memset", z_[:], 0.0, deps=[acc_done]) for z_ in zt]
                bshare = dict(bcs_free=[])
                ress = []
                for k in range(2):
                    ress.append(dict(rd=zt[k], bank=banks[k], bcs=(bcs if k % 2 == 0 else bcs2), rd_free=[t_z[k]], bcs_free=[]))
                hb_toks = {}
                nctr = 0
                for hh in range(4):
                    for c in range(4):
                        cs = slice(c * 512, (c + 1) * 512)
                        if hh % 2 == 0:
                            dst = OAT[0:64, hh // 2, cs]
                            extra = []
                        else:
                            dst = tmpB[0:64, cs]
                            extra = state["tmpB_free"]
                        res = ress[nctr % 2]
                        nctr += 1

                        def done(t_mul, hh=hh, c=c):
                            if hh % 2 == 1:
                                hb_toks.setdefault(hh, []).append(t_mul)
                                if c == 3:
                                    tk = sp.dma(OAT[64:128, hh // 2, :], tmpB[0:64, :], mvslot, deps=[hb_toks[hh]])
                                    state["tmpB_free"] = [tk]
                        normalize(ACC[0:64, hh, cs], ACC[64:65, hh, cs], dst, [acc_done, extra], done, res=res)
                        tick()
                flush()
                barrier([(mvslot.sem, mvslot.n)])

            if dbg:
                d_oat = dout("d_oat", [128, 2 * S])
                with contextlib.ExitStack() as sd:
                    oaf = sb(sd, "dbg_oaf", [128, 2 * S], F32)
                    t1 = dve.op("tensor_copy", out=oaf[:], in_=OAT[:].rearrange("p k n -> p (k n)"))
                    dsl = new_slot()
                    sp.dma(d_oat, oaf[:], dsl, deps=[t1])
                    sp.wait([(dsl.sem, dsl.n)])
                    barrier()

            with contextlib.ExitStack() as s2:
                gpost = sb(s2, "gpost", [128, D], F32)
                gfpre = sb(s2, "gfpre", [128, D], F32)
                gfpost = sb(s2, "gfpost", [128, D], F32)
                x1h = sb(s2, "x1h", [128, 8, D], F32)
                ssq = sb(s2, "ssq", [128, 64], F32)
                sml = sb(s2, "sml", [128, 64], F32)
                rms_ctr = [0]
                g2slot = new_slot()
                sp.dma(gpost[:], g_post.partition_broadcast(128), g2slot)
                sp.dma(gfpre[:], g_fpre.partition_broadcast(128), g2slot)
                t_gains = sp.dma(gfpost[:], g_fpost.partition_broadcast(128), g2slot)
                out_slots = [new_slot() for _ in range(4)]
                out_toks = [None, None, None, None]
                xt_slots = [new_slot() for _ in range(2)]
                wo_slot = new_slot()


                def soft_barrier(extra=()):
                    ta = (act.sem, act.n)
                    td = (dve.sem, dve.n)
                    tp = (pe.sem, pe.n)
                    ex = list(extra)
                    act.wait([td, tp, ex])
                    dve.wait([ta, tp, ex])
                    pool.wait([ta, td, tp, ex])
                    sp.wait([ta, td, tp, ex])

                def rms_rstd(sq_bank_aps, junk_t, deps, col):
                    n = len(sq_bank_aps)
                    col = 2 * (rms_ctr[0] % 16) + col
                    rms_ctr[0] += 1
                    t = None
                    for k, ap in enumerate(sq_bank_aps):
                        t = act.op("activation", out=junk_t[:, 0:ap.shape[-1]], in_=ap, func=AF.Square, accum_out=ssq[:, 2 * col + k:2 * col + k + 1],
                                   deps=[deps])
                    if n == 2:
                        t = dve.op("tensor_tensor", out=ssq[:, 2 * col:2 * col + 1], in0=ssq[:, 2 * col:2 * col + 1],
                                   in1=ssq[:, 2 * col + 1:2 * col + 2], op=ALU.add, deps=[t])
                    t = act.op("activation", out=sml[:, 2 * col:2 * col + 1], in_=ssq[:, 2 * col:2 * col + 1], func=AF.Ln, scale=1.0 / D,
                               bias=EPS, deps=[t])
                    t = act.op("activation", out=sml[:, 2 * col + 1:2 * col + 2], in_=sml[:, 2 * col:2 * col + 1], func=AF.Exp, scale=-0.5,
                               deps=[t])
                    return t, sml[:, 2 * col + 1:2 * col + 2]

                h2T_tok = {}
                e_pre_all = {}
                for p in range(2):
                    with contextlib.ExitStack() as scd:
                        mergedTh = sb(scd, "mergedTh", [128, KC, 1024], BF16)
                        Wout = sb(scd, "Wout", [128, KC, D], BF16)
                        sa_t = [sb(scd, f"sa{i}", [128, 512], F32) for i in range(2)]
                        sb_t = [sb(scd, f"sbt{i}", [128, 512], F32) for i in range(2)]
                        xt = [sb(scd, f"xt{i}", [128, D], F32) for i in range(2)]
                        n1 = sb(scd, "n1", [128, D], F32)
                        h2b = [sb(scd, f"h2b{i}", [128, D], BF16) for i in range(4)]
                        junk = sb(scd, "junkD", [128, D], BF16)
                        cw_tok = None
                        sa_free = [[], []]
                        kctr = 0
                        m_toks = []
                        for ocg in range(4):
                            wga_t, t_ga, rel_ga = get_full(("C", p, ocg, "a"), w_in_v[:, :, C_GA + ocg * 256:C_GA + (ocg + 1) * 256])
                            wgb_t, t_gb, rel_gb = get_full(("C", p, ocg, "b"), w_in_v[:, :, C_GB + ocg * 256:C_GB + (ocg + 1) * 256])
                            if ocg == 1:
                                cw_tok = [pool.dma(Wout[:], w_o_v, wo_slot)]
                            for ocl in range(2):
                                oc = 2 * ocg + ocl
                                for cl in range(2):
                                    c = 2 * p + cl
                                    cs = slice(c * 512, (c + 1) * 512)
                                    base = 4 * (kctr % 2)
                                    ri = kctr % 2
                                    kctr += 1
                                    bGA, bGB, bPA, bPB = banks[base], banks[base + 1], banks[base + 2], banks[base + 3]
                                    tGA = tGB = tPA = tPB = None
                                    pe.wait(bGA.free)
                                    for kc in range(KC):
                                        tGA = pe.op("matmul", bGA.t[:], lhsT=wga_t[:, kc, ocl * 128:(ocl + 1) * 128], rhs=hT[:, kc, cs],
                                                    start=(kc == 0), stop=(kc == KC - 1), sig=(kc == KC - 1), deps=[t_ga])
                                    pe.wait(bGB.free)
                                    for kc in range(KC):
                                        tGB = pe.op("matmul", bGB.t[:], lhsT=wgb_t[:, kc, ocl * 128:(ocl + 1) * 128], rhs=hT[:, kc, cs],
                                                    start=(kc == 0), stop=(kc == KC - 1), sig=(kc == KC - 1), deps=[t_gb])
                                    pe.wait(bPA.free)
                                    for k2 in range(2):
                                        tPA = pe.op("matmul", bPA.t[:], lhsT=WA[:, k2, oc * 128:(oc + 1) * 128], rhs=OAT[:, k2, cs],
                                                    start=(k2 == 0), stop=(k2 == 1), sig=(k2 == 1), deps=[ab_tok])
                                    pe.wait(bPB.free)
                                    for k4 in range(4):
                                        tPB = pe.op("matmul", bPB.t[:], lhsT=WB[:, k4, oc * 128:(oc + 1) * 128], rhs=OBT[:, k4, cs],
                                                    start=(k4 == 0), stop=(k4 == 3), sig=(k4 == 3))
                                    rel_ga([tGA])
                                    rel_gb([tGB])
                                    t_sa = act.op("activation", out=sa_t[ri][:], in_=bGA.t[:], func=AF.Sigmoid, deps=[tGA, sa_free[ri]])
                                    t_sb = act.op("activation", out=sb_t[ri][:], in_=bGB.t[:], func=AF.Sigmoid, deps=[tGB])
                                    t_m1 = dve.op("tensor_tensor", out=sa_t[ri][:], in0=bPA.t[:], in1=sa_t[ri][:], op=ALU.mult, deps=[tPA, t_sa])
                                    t_m2 = dve.op("tensor_tensor", out=sb_t[ri][:], in0=bPB.t[:], in1=sb_t[ri][:], op=ALU.mult, deps=[tPB, t_sb])
                                    bGA.free = [t_m2]
                                    bGB.free = [t_m2]
                                    bPA.free = [t_m2]
                                    bPB.free = [t_m2]
                                    t_mg = dve.op("tensor_tensor", out=mergedTh[:, oc, cl * 512:(cl + 1) * 512], in0=sa_t[ri][:], in1=sb_t[ri][:],
                                                  op=ALU.add, deps=[t_m1, t_m2])
                                    sa_free[ri] = [t_mg]
                                    m_toks.append(t_mg)
                        if dbg and p == 0:
                            d_merged = dout("d_mergedh", [128, KC * 1024])
                            with contextlib.ExitStack() as sd:
                                mf = sb(sd, "dbg_mf", [128, 1024], F32)
                                dsl = new_slot()
                                tk = None
                                for kc in range(KC):
                                    t1 = dve.op("tensor_copy", out=mf[:], in_=mergedTh[:, kc, :], deps=[m_toks, tk])
                                    tk = sp.dma(d_merged[:, kc * 1024:(kc + 1) * 1024], mf[:], dsl, deps=[t1])
                                barrier([tk])
                        merged_ready = [(dve.sem, dve.n)]
                        for jg_ in range(2):
                            pre_issue(("E", p, jg_, "g"), w_g_v[:, :, jg_ * 256:(jg_ + 1) * 256])
                            pre_issue(("E", p, jg_, "u"), w_u_v[:, :, jg_ * 256:(jg_ + 1) * 256])
                        xt_free = [[], []]
                        h2b_free = [[], [], [], []]
                        n1_free = []
                        infos = {}

                        def stage_X(tl):
                            t = 8 * p + tl
                            i = tl % 2
                            i4 = tl % 4
                            t_x = sp.dma(xt[i][:], x[t * 128:(t + 1) * 128, :], xt_slots[i], deps=[xt_free[i]])
                            m3i = tl % 3
                            bM = [banks[2 * m3i], banks[2 * m3i + 1]]
                            pe.wait([bM[0].free, bM[1].free, merged_ready, cw_tok])
                            tM = [None, None]
                            for hc in range(2):
                                for kc in range(KC):
                                    tM[hc] = pe.op("matmul", bM[hc].t[:], lhsT=mergedTh[:, kc, tl * 128:(tl + 1) * 128],
                                                   rhs=Wout[:, kc, hc * 512:(hc + 1) * 512], start=(kc == 0), stop=(kc == KC - 1),
                                                   sig=(kc == KC - 1))
                            t_r, rs_ap = rms_rstd([bM[0].t[:], bM[1].t[:]], junk, [tM[0], tM[1]], 0)
                            t_n = None
                            for hc in range(2):
                                t_n = dve.op("scalar_tensor_tensor", out=n1[:, hc * 512:(hc + 1) * 512], in0=bM[hc].t[:], scalar=rs_ap,
                                             in1=gpost[:, hc * 512:(hc + 1) * 512], op0=ALU.mult, op1=ALU.mult,
                                             deps=[t_r, t_gains, state.get("n1_free", [])])
                                bM[hc].free = [t_n]
                            t_x1 = dve.op("tensor_tensor", out=x1h[:, tl, :], in0=xt[i][:], in1=n1[:], op=ALU.add, deps=[t_x, t_n])
                            xt_free[i] = [t_x1]
                            state["n1_free"] = [t_x1]
                            infos[tl] = dict(t=t, i=i, t_x1=t_x1)

                        def stage_Y(tl):
                            inf = infos[tl]
                            i = tl % 4
                            t_r2, rs2_ap = rms_rstd([x1h[:, tl, :]], junk, [inf["t_x1"]], 1)
                            inf["t_h2"] = dve.op("scalar_tensor_tensor", out=h2b[i][:], in0=x1h[:, tl, :], scalar=rs2_ap, in1=gfpre[:],
                                                 op0=ALU.mult, op1=ALU.mult, deps=[t_r2, h2b_free[i]])

                        def emit_T(tl):
                            inf = infos[tl]
                            t, i = inf["t"], tl % 4
                            bk = banks[6 + tl % 2]
                            t_tr = transposes_to(bk, h2b[i], [inf["t_h2"]])
                            h2b_free[i] = [t_tr]
                            if tl % 2 == 0:
                                t_cp = act.op("activation", out=hT[:, :, t * 128:(t + 1) * 128],
                                              in_=bank_bf16(bk).rearrange("p (k n) -> p k n", k=KC), func=AF.Copy, deps=[t_tr])
                            else:
                                t_cp = dve.op("tensor_copy", out=hT[:, :, t * 128:(t + 1) * 128],
                                              in_=bank_bf16(bk).rearrange("p (k n) -> p k n", k=KC), deps=[t_tr])
                            bk.free = [t_cp]
                            h2T_tok[t] = t_cp

                        e_pre = {}
                        e_w0 = {}

                        def pre_E_unit(jl, kctr_):
                            if not e_w0:
                                e_w0["g"] = get_full(("E", p, 0, "g"), w_g_v[:, :, 0:256])
                                e_w0["u"] = get_full(("E", p, 0, "u"), w_u_v[:, :, 0:256])
                            wg_t_, t_wg_, _ = e_w0["g"]
                            wu_t_, t_wu_, _ = e_w0["u"]
                            c_ = 2 * p
                            cs_ = slice(c_ * 512, (c_ + 1) * 512)
                            bG_, bU_ = banks[2 * (kctr_ % 4)], banks[2 * (kctr_ % 4) + 1]
                            pe.wait([bG_.free, bU_.free])
                            tG_ = tU_ = None
                            for kc in range(KC):
                                tG_ = pe.op("matmul", bG_.t[:], lhsT=wg_t_[:, kc, jl * 128:(jl + 1) * 128], rhs=hT[:, kc, cs_],
                                            start=(kc == 0), stop=(kc == KC - 1), sig=(kc == KC - 1),
                                            deps=[t_wg_, [h2T_tok[4 * c_ + q_] for q_ in range(4)]])
                            for kc in range(KC):
                                tU_ = pe.op("matmul", bU_.t[:], lhsT=wu_t_[:, kc, jl * 128:(jl + 1) * 128], rhs=hT[:, kc, cs_],
                                            start=(kc == 0), stop=(kc == KC - 1), sig=(kc == KC - 1), deps=[t_wu_])
                            e_pre[(jl, 0)] = (tG_, tU_)

                        for step in range(8 + 3):
                            if step < 8:
                                stage_X(step)
                            if 0 <= step - 1 < 8:
                                stage_Y(step - 1)
                            if step == 9:
                                pre_E_unit(0, 0)
                                pre_E_unit(1, 2)
                            if 0 <= step - 3 < 8:
                                emit_T(step - 3)
                        e_pre_all[p] = (e_pre, e_w0)
                        h2T_ready = [(act.sem, act.n), (dve.sem, dve.n)]
                        soft_barrier()
                    if dbg and p == 0:
                        d_x1 = dout("d_x1h", [128, 8 * D])
                        dsl = new_slot()
                        tk = sp.dma(d_x1, x1h[:].rearrange("p t n -> p (t n)"), dsl)
                        barrier([tk])
                    with contextlib.ExitStack() as sef:
                        uTh = sb(sef, "uTh", [128, NJ, 1024], BF16)
                        sg_t = [sb(sef, f"sg{i}", [128, 512], F32) for i in range(2)]
                        ot = [sb(sef, f"ot{i}", [128, D], F32) for i in range(4)]
                        junk = sb(sef, "junkF", [128, D], BF16)
                        sg_free = [[], []]
                        kctr = 0
                        for jg in range(11):
                            e_pre, e_w0 = e_pre_all.get(p, ({}, {}))
                            if jg == 0 and e_w0:
                                wg_t, t_wg, rel_g = e_w0["g"]
                                wu_t, t_wu, rel_u = e_w0["u"]
                            else:
                                wg_t, t_wg, rel_g = get_full(("E", p, jg, "g"), w_g_v[:, :, jg * 256:(jg + 1) * 256])
                                wu_t, t_wu, rel_u = get_full(("E", p, jg, "u"), w_u_v[:, :, jg * 256:(jg + 1) * 256])
                            for jl in range(2):
                                j = 2 * jg + jl
                                for cl in range(2):
                                    c = 2 * p + cl
                                    cs = slice(c * 512, (c + 1) * 512)
                                    base = 2 * (kctr % 4)
                                    ri = kctr % 2
                                    kctr += 1
                                    bG, bU = banks[base], banks[base + 1]
                                    if jg == 0 and (jl, cl) in e_pre:
                                        tG, tU = e_pre[(jl, cl)]
                                    else:
                                        pe.wait([bG.free, bU.free])
                                        tG = tU = None
                                        for kc in range(KC):
                                            tG = pe.op("matmul", bG.t[:], lhsT=wg_t[:, kc, jl * 128:(jl + 1) * 128], rhs=hT[:, kc, cs],
                                                       start=(kc == 0), stop=(kc == KC - 1), sig=(kc == KC - 1),
                                                       deps=[t_wg, [h2T_tok[4 * c + q_] for q_ in range(4)]])
                                        for kc in range(KC):
                                            tU = pe.op("matmul", bU.t[:], lhsT=wu_t[:, kc, jl * 128:(jl + 1) * 128], rhs=hT[:, kc, cs],
                                                       start=(kc == 0), stop=(kc == KC - 1), sig=(kc == KC - 1), deps=[t_wu])
                                    rel_g([tG])
                                    rel_u([tU])
                                    t_sg = act.op("activation", out=sg_t[ri][:], in_=bG.t[:], func=AF.Silu, deps=[tG, sg_free[ri]])
                                    t_u = dve.op("tensor_tensor", out=uTh[:, j, cl * 512:(cl + 1) * 512], in0=bU.t[:], in1=sg_t[ri][:],
                                                 op=ALU.mult, deps=[tU, t_sg])
                                    bG.free = [t_u]
                                    bU.free = [t_u]
                                    sg_free[ri] = [t_u]
                        u_ready = [(dve.sem, dve.n)]
                        for cl in range(2):
                            pe.wait(u_ready)
                            tF = [None] * 8
                            NPRE = 6
                            order = [(kc, t4) for t4 in range(4) for kc in range(NPRE)] + \
                                    [(kc, t4) for kc in range(NPRE, NJ) for t4 in range(4)]
                            wd_loaded = {}
                            for kc_ in range(NPRE):
                                wd_loaded[kc_] = ring_half(w_d[kc_ * 128:(kc_ + 1) * 128, :])
                            for kc, t4 in order:
                                if kc not in wd_loaded:
                                    wd_loaded[kc] = ring_half(w_d[kc * 128:(kc + 1) * 128, :])
                                wd_t, t_w, rel_d = wd_loaded[kc]
                                tl = 4 * cl + t4
                                for hc in range(2):
                                    bk_ = banks[2 * t4 + hc]
                                    if kc == 0:
                                        pe.wait(bk_.free)
                                    tF[2 * t4 + hc] = pe.op("matmul", bk_.t[:], lhsT=uTh[:, kc, tl * 128:(tl + 1) * 128],
                                                            rhs=wd_t[:, hc * 512:(hc + 1) * 512], start=(kc == 0), stop=(kc == NJ - 1),
                                                            sig=(kc == NJ - 1 or (t4 == 3 and hc == 1)), deps=[t_w])
                                if t4 == 3:
                                    rel_d([tF[7]])
                            t_ns = []
                            for t4 in range(4):
                                b0, b1 = banks[2 * t4], banks[2 * t4 + 1]
                                t_r, rs_ap = rms_rstd([b0.t[:], b1.t[:]], junk, [tF[2 * t4], tF[2 * t4 + 1]], 0)
                                t_n = None
                                for hc, bb in enumerate((b0, b1)):
                                    t_n = dve.op("scalar_tensor_tensor", out=ot[t4][:, hc * 512:(hc + 1) * 512], in0=bb.t[:], scalar=rs_ap,
                                                 in1=gfpost[:, hc * 512:(hc + 1) * 512], op0=ALU.mult, op1=ALU.mult,
                                                 deps=[t_r, t_gains, out_toks[t4]])
                                    bb.free = [t_n]
                                t_ns.append(t_n)
                            for t4 in range(4):
                                tl = 4 * cl + t4
                                t = 8 * p + tl
                                t_o = dve.op("tensor_tensor", out=ot[t4][:], in0=ot[t4][:], in1=x1h[:, tl, :], op=ALU.add, deps=[t_ns[t4]])
                                out_toks[t4] = sp.dma(out[t * 128:(t + 1) * 128, :], ot[t4][:], out_slots[t4], deps=[t_o])
                        if p == 0:
                            for ocg_ in range(2):
                                pre_issue(("C", 1, ocg_, "a"), w_in_v[:, :, C_GA + ocg_ * 256:C_GA + (ocg_ + 1) * 256])
                                pre_issue(("C", 1, ocg_, "b"), w_in_v[:, :, C_GB + ocg_ * 256:C_GB + (ocg_ + 1) * 256])
                        soft_barrier([tk_ for tk_ in out_toks if tk_ is not None])

        barrier()
    return nc, dbg_outs


def make_in_maps(inputs):
    c = host_consts()
    x = np.ascontiguousarray(inputs["x"], dtype=np.float32)
    shared = dict(
        w_in=np.ascontiguousarray(inputs["w_in"][0]),
        w_a=np.ascontiguousarray(inputs["w_proj_a"][0]),
        w_b=np.ascontiguousarray(inputs["w_proj_b"][0]),
        w_o=np.ascontiguousarray(inputs["w_out"][0]),
        b_f=np.ascontiguousarray(inputs["b_forget"][0].reshape(8, 1)),
        w_g=np.ascontiguousarray(inputs["w_ffn_gate"][0]),
        w_u=np.ascontiguousarray(inputs["w_ffn_up"][0]),
        w_d=np.ascontiguousarray(inputs["w_ffn_down"][0]),
        g_pre=np.ascontiguousarray(inputs["norm_mix_pre"][0].reshape(1, D)),
        g_post=np.ascontiguousarray(inputs["norm_mix_post"][0].reshape(1, D)),
        g_fpre=np.ascontiguousarray(inputs["norm_ffn_pre"][0].reshape(1, D)),
        g_fpost=np.ascontiguousarray(inputs["norm_ffn_post"][0].reshape(1, D)),
        **c,
    )
    return [dict(x=x[b], **shared) for b in range(x.shape[0])]


def kernel(**inputs):
    nc, _ = build(False)
    in_maps = make_in_maps(inputs)
    res = run_bass_kernel_spmd(nc, in_maps, core_ids=list(range(8)))
    return np.stack([np.asarray(r["out"], dtype=np.float32) for r in res.results], axis=0)
```

```python
import contextlib
import os
import numpy as np
import ml_dtypes
import concourse.bass as bass
import concourse.mybir as mybir
from concourse.bass_utils import run_bass_kernel_spmd

F32 = mybir.dt.float32
BF16 = mybir.dt.bfloat16
AF = mybir.ActivationFunctionType
ALU = mybir.AluOpType

S = 2048
D = 1024
NT = 16
KC = 8
DFF = 2816
NJ = 22
INC = 5896
EPS = 1e-6
C_QA, C_KA, C_VA, C_QB, C_KB, C_VB, C_F, C_GA, C_GB = 0, 768, 1536, 2304, 2816, 3328, 3840, 3848, 4872
DIL = (1, 4, 16)
NEG = -30000.0
RECIP_FAST = os.environ.get('RECIP_EXACT') is None


class Slot:
    def __init__(self, nc, st, name):
        self.sem = st.enter_context(nc.semaphore(name))
        self.n = 0


class Eng:
    def __init__(self, nc, st, raw, name):
        self.raw = raw
        self.name = name
        self.sem = st.enter_context(nc.semaphore("sem_" + name))
        self.n = 0
        self.seen = {}

    def wait(self, deps):
        for tok in deps:
            if tok is None:
                continue
            if isinstance(tok, list):
                self.wait(tok)
                continue
            sem, val = tok
            k = id(sem)
            if self.seen.get(k, 0) < val:
                self.raw.wait_ge(sem, val)
                self.seen[k] = val

    def op(self, meth, *a, deps=(), sig=True, **kw):
        self.wait(deps)
        ins = getattr(self.raw, meth)(*a, **kw)
        if sig:
            self.n += 1
            ins.then_inc(self.sem, 1)
            return (self.sem, self.n)
        return None

    def dma(self, out, in_, slot, deps=()):
        self.wait(deps)
        self.raw.dma_start(out=out, in_=in_).then_inc(slot.sem, 16)
        slot.n += 16
        return (slot.sem, slot.n)


class Bank:
    def __init__(self, t):
        self.t = t
        self.free = []


def host_consts():
    ident = np.eye(128, dtype=np.float32)
    p = np.arange(128)[:, None]
    f = np.arange(128)[None, :]
    masku = np.where(f >= p, 0.0, NEG).astype(np.float32)
    maskl = np.where(p >= f, 0.0, NEG).astype(np.float32)
    mask2 = np.concatenate([maskl, masku], axis=1)
    perm = np.zeros((128, 128), np.float32)
    for m in range(128):
        dm = m % 64
        if dm < 8:
            perm[m + 8, m] = 1.0
        elif dm < 16:
            perm[m - 8, m] = 1.0
    half = 8
    inv_freq = np.power(np.float32(500000.0), -np.arange(0, 16, 2, dtype=np.float32) / np.float32(16)).astype(np.float32)
    pos = np.arange(S, dtype=np.float32)
    ang = (pos[:, None] * inv_freq[None, :]).astype(np.float32)
    cos = np.cos(ang.astype(np.float64)).astype(np.float32).T
    sin = np.sin(ang.astype(np.float64)).astype(np.float32).T
    cosf = np.ones((128, S), np.float32)
    sinf = np.zeros((128, S), np.float32)
    for m in range(128):
        dm = m % 64
        if dm < 8:
            cosf[m] = cos[dm]
            sinf[m] = -sin[dm]
        elif dm < 16:
            cosf[m] = cos[dm - 8]
            sinf[m] = sin[dm - 8]
    rows = np.concatenate([np.ones((3, S), np.float32), -np.ones((3, S), np.float32)], axis=0)
    return dict(c_ident=ident, c_mask2=mask2, c_perm=perm, c_cosf=cosf, c_sinf=sinf, c_rows=rows)


def build(dbg=False):
    nc = bass.Bass("TRN2", target_bir_lowering=False)

    def din(name, shape):
        return nc.dram_tensor(name, list(shape), F32, kind="ExternalInput").ap()

    x = din("x", [S, D])
    w_in = din("w_in", [D, INC])
    w_a = din("w_a", [256, D])
    w_b = din("w_b", [512, D])
    w_o = din("w_o", [D, D])
    b_f = din("b_f", [8, 1])
    w_g = din("w_g", [D, DFF])
    w_u = din("w_u", [D, DFF])
    w_d = din("w_d", [DFF, D])
    g_pre = din("g_pre", [1, D])
    g_post = din("g_post", [1, D])
    g_fpre = din("g_fpre", [1, D])
    g_fpost = din("g_fpost", [1, D])
    c_ident = din("c_ident", [128, 128])
    c_mask2 = din("c_mask2", [128, 256])
    c_perm = din("c_perm", [128, 128])
    c_cosf = din("c_cosf", [128, S])
    c_sinf = din("c_sinf", [128, S])
    c_rows = din("c_rows", [6, S])
    out = nc.dram_tensor("out", [S, D], F32, kind="ExternalOutput").ap()
    dbg_outs = {}

    def dout(name, shape):
        t = nc.dram_tensor(name, list(shape), F32, kind="ExternalOutput").ap()
        dbg_outs[name] = t
        return t

    w_in_v = w_in.rearrange("(kc p) n -> p kc n", p=128)
    w_g_v = w_g.rearrange("(kc p) n -> p kc n", p=128)
    w_u_v = w_u.rearrange("(kc p) n -> p kc n", p=128)
    w_o_v = w_o.rearrange("(kc p) n -> p kc n", p=128)
    w_a_v = w_a.rearrange("(kc p) n -> p kc n", p=128)
    w_b_v = w_b.rearrange("(kc p) n -> p kc n", p=128)

    with contextlib.ExitStack() as st:
        E = st.enter_context
        pe = Eng(nc, st, nc.tensor, "pe")
        act = Eng(nc, st, nc.scalar, "act")
        dve = Eng(nc, st, nc.vector, "dve")
        pool = Eng(nc, st, nc.gpsimd, "pool")
        sp = Eng(nc, st, nc.sync, "sp")
        engines = [pe, act, dve, pool, sp]
        slot_id = [0]

        def new_slot(stk=st):
            slot_id[0] += 1
            return Slot(nc, stk, f"ds{slot_id[0]}")

        def sb(stk, name, shape, dt):
            slot_id[0] += 1
            return stk.enter_context(nc.sbuf_tensor(f"{name}_{slot_id[0]}", list(shape), dt))

        def barrier(extra=()):
            toks = list(extra)
            for e in (pe, act, dve):
                toks.append(e.op("drain"))
            for e in engines:
                e.wait(toks)

        dbl = [E(nc.psum_tensor(f"ps{i}", [128, 1024], F32)) for i in range(4)]
        banks = [Bank(dbl[i // 2][:, (i % 2) * 512:(i % 2 + 1) * 512]) for i in range(8)]

        def bank_bf16(b):
            return b.t[:].bitcast(BF16)

        hT = sb(st, "hT", [128, KC, S], BF16)
        ident = sb(st, "ident", [128, 128], BF16)
        mask2 = sb(st, "mask2", [128, 256], BF16)
        perm = sb(st, "perm", [128, 128], BF16)
        onesrow = sb(st, "onesrow", [128, 128], F32)
        cslot = new_slot()
        ctoks = []
        ctoks.append(pool.dma(ident[:], c_ident, cslot))
        ctoks.append(pool.dma(mask2[:], c_mask2, cslot))
        ctoks.append(pool.dma(perm[:], c_perm, cslot))
        dve.op("memset", onesrow[:], 0.0, sig=False)
        tok_ones = dve.op("memset", onesrow[64:65, :], 1.0)
        const_tok = [ctoks[-1], tok_ones]

        def transposes_to(bank, src_tile, deps):
            pe.wait(bank.free)
            pe.wait(deps)
            pe.wait(const_tok)
            bv = bank_bf16(bank)
            tok = None
            for kc in range(KC):
                tok = pe.op("transpose", out=bv[:, kc * 128:(kc + 1) * 128], in_=src_tile[:, kc * 128:(kc + 1) * 128],
                            identity=ident[:], sig=(kc == KC - 1))
            return tok

        with contextlib.ExitStack() as sa:
            xall = sb(sa, "xall", [128, NT, D], F32)
            gpre = sb(sa, "gpre", [128, D], F32)
            ss = sb(sa, "ss", [128, NT], F32)
            lnv = sb(sa, "lnv", [128, NT], F32)
            rstd = sb(sa, "rstd", [128, NT], F32)
            junk = sb(sa, "junkA", [128, D], BF16)
            hb = [sb(sa, f"hb{i}", [128, D], BF16) for i in range(2)]
            gslot = new_slot()
            tok_g = sp.dma(gpre[:], g_pre.partition_broadcast(128), gslot)
            xslots = [new_slot() for _ in range(NT)]
            tokx = [sp.dma(xall[:, t, :], x[t * 128:(t + 1) * 128, :], xslots[t]) for t in range(NT)]
            hb_free = [[], []]
            pend_cp = None

            def emit_cp(t, bk, t_tr):
                dstv = hT[:, :, t * 128:(t + 1) * 128]
                srcv = bank_bf16(bk).rearrange("p (k n) -> p k n", k=KC)
                t_cp = dve.op("tensor_copy", out=dstv, in_=srcv, deps=[t_tr])
                bk.free = [t_cp]

            for t in range(NT):
                i = t % 2
                tsq = act.op("activation", out=junk[:], in_=xall[:, t, :], func=AF.Square, accum_out=ss[:, t:t + 1],
                             deps=[tokx[t]])
                t_ln = act.op("activation", out=lnv[:, t:t + 1], in_=ss[:, t:t + 1], func=AF.Ln, scale=1.0 / D, bias=EPS, deps=[tsq])
                t_rs = act.op("activation", out=rstd[:, t:t + 1], in_=lnv[:, t:t + 1], func=AF.Exp, scale=-0.5, deps=[t_ln])
                t_h = dve.op("scalar_tensor_tensor", out=hb[i][:], in0=xall[:, t, :], scalar=rstd[:, t:t + 1], in1=gpre[:],
                             op0=ALU.mult, op1=ALU.mult, deps=[t_rs, tok_g, hb_free[i]])
                bk = banks[t % 4]
                t_tr = transposes_to(bk, hb[i], [t_h])
                hb_free[i] = [t_tr]
                if pend_cp is not None:
                    emit_cp(*pend_cp)
                pend_cp = (t, bk, t_tr)
            emit_cp(*pend_cp)
            barrier()
        if dbg:
            d_hT = dout("d_hT", [128, KC * S])
            dslot = new_slot()
            with contextlib.ExitStack() as sd:
                hTf = sb(sd, "dbg_hTf", [128, KC * S], F32)
                t1 = dve.op("tensor_copy", out=hTf[:], in_=hT[:].rearrange("p k n -> p (k n)"))
                sp.dma(d_hT, hTf[:], dslot, deps=[t1])
                sp.wait([(dslot.sem, dslot.n)])
                barrier()

        S_banks = [banks[0], banks[1], banks[2]]
        O_banks = [banks[3], banks[4]]
        BCb = banks[5]
        PJ = [banks[6], banks[7]]

        def proj_fm(evac, wtile, col0, deps_w, pj=None):
            for c in range(4):
                bk = (pj or PJ)[c % 2]
                pe.wait(bk.free)
                tok = None
                for kc in range(KC):
                    tok = pe.op("matmul", bk.t[:], lhsT=wtile[:, kc, col0:col0 + 128], rhs=hT[:, kc, c * 512:(c + 1) * 512],
                                start=(kc == 0), stop=(kc == KC - 1), sig=(kc == KC - 1), deps=(deps_w if kc == 0 else ()))
                evac(c, bk, tok)

        with contextlib.ExitStack() as sbc:
            OBT = sb(sbc, "OBT", [128, 4, S], BF16)
            OAT = sb(sbc, "OAT", [128, 2, S], BF16)
            wring = [sb(sbc, f"wring{i}", [128, KC, 256], BF16) for i in range(4)]
            wf = sb(sbc, "wf", [128, KC, 8], BF16)
            fslot = new_slot()
            t_wf = pool.dma(wf[:], w_in_v[:, :, C_F:C_F + 8], fslot)
            WA = sb(sbc, "WA", [128, 2, D], BF16)
            WB = sb(sbc, "WB", [128, 4, D], BF16)
            ab_slot = new_slot()
            wr_free = [[] for _ in range(8)]
            wr_slot = [new_slot() for _ in range(8)]
            wr_ctr = [0]
            wr_rel = [True] * 8

            def ring_full(src_ap):
                if wr_ctr[0] % 2:
                    wr_ctr[0] += 1
                h0 = wr_ctr[0] % 8
                wr_ctr[0] += 2
                assert wr_rel[h0] and wr_rel[h0 + 1], "ring slot reused before its release was recorded"
                wr_rel[h0] = wr_rel[h0 + 1] = False
                tile_ = wring[h0 // 2]
                tok = pool.dma(tile_[:], src_ap, wr_slot[h0], deps=[wr_free[h0], wr_free[h0 + 1]])

                def release(tokens):
                    wr_free[h0] = tokens
                    wr_free[h0 + 1] = tokens
                    wr_rel[h0] = wr_rel[h0 + 1] = True
                return tile_, tok, release

            def ring_half(src_ap):
                h0 = wr_ctr[0] % 8
                wr_ctr[0] += 1
                assert wr_rel[h0], "ring half-slot reused before its release was recorded"
                wr_rel[h0] = False
                view = wring[h0 // 2][:, 4 * (h0 % 2):4 * (h0 % 2) + 4, :].rearrange("p a b -> p (a b)")
                tok = pool.dma(view, src_ap, wr_slot[h0], deps=[wr_free[h0]])

                def release(tokens):
                    wr_free[h0] = tokens
                    wr_rel[h0] = True
                return view, tok, release

            ring_pre = {}

            def pre_issue(key, src_ap):
                ring_pre[key] = ring_full(src_ap)

            def get_full(key, src_ap):
                if key in ring_pre:
                    return ring_pre.pop(key)
                return ring_full(src_ap)

            mvslot = new_slot()
            state = dict(rd_free=[], bcs_free=[], tmpB_free=[], n1_free=[])
            pending = []

            def tick():
                for p_ in pending:
                    p_[0] -= 1
                while pending and pending[0][0] <= 0:
                    pending.pop(0)[1]()

            def flush():
                while pending:
                    pending.pop(0)[1]()

            def normalize(src_ap_num, src_ap_den, dst, deps, on_done, res=None):
                if res is None:
                    res = state["res0"]
                rdt, bcb, bct = res["rd"], res["bank"], res["bcs"]
                if res.get("dve_recip"):
                    t_ex = dve.op("reciprocal", out=rdt[64:65, :], in_=src_ap_den, deps=[deps, res["rd_free"]])
                else:
                    t_ln = act.op("activation", out=rdt[64:65, :], in_=src_ap_den, func=AF.Ln, deps=[deps, res["rd_free"]])
                    t_ex = act.op("activation", out=rdt[64:65, :], in_=rdt[64:65, :], func=AF.Exp, scale=-1.0, deps=[t_ln])

                def later():
                    if res.get("take"):
                        _d, bA, bB = res["take"]()
                        pe.wait([bA.free, bB.free])
                        bcb_ = bA
                    else:
                        bA = bB = None
                        bcb_ = bcb
                        pe.wait(bcb_.free)
                    t_bc = pe.op("matmul", bcb_.t[:, :], lhsT=onesrow[:, :], rhs=rdt[:, :], start=True, stop=True,
                                 deps=[t_ex, tok_ones])
                    res["rd_free"] = [t_bc]
                    t_cp = dve.op("tensor_copy", out=bct[:, :], in_=bcb_.t[0:64, :], deps=[t_bc, res["bcs_free"]])
                    bcb_.free = [t_cp]
                    if bB is not None:
                        bB.free = [t_cp]
                    t_mul = dve.op("tensor_tensor", out=dst, in0=src_ap_num, in1=bct[:, :], op=ALU.mult, deps=[t_cp, deps])
                    res["bcs_free"] = [t_mul]
                    on_done(t_mul)
                pending.append([2, later])

            with contextlib.ExitStack() as sf:
                rd = sb(sf, "rd_f", [128, 512], F32)
                bcs = sb(sf, "bcs_f", [64, 512], F32)
                tmpB = sb(sf, "tmpB_f", [64, S], BF16)
                spl = [sb(sf, f"spl{i}", [8, S], BF16) for i in range(3)]
                if True:
                    bft = sb(sf, "bft", [8, 1], F32)
                    nbf = sb(sf, "nbf", [8, 1], F32)
                    ev = sb(sf, "ev", [8, S], F32)
                    ones8 = sb(sf, "ones8", [8, S], BF16)
                    G = sb(sf, "G", [8, S], F32)
                    fslot2 = new_slot()
                    t_bf = sp.dma(bft[:], b_f, fslot2)
                    t_nb = dve.op("tensor_scalar", out=nbf[:], in0=bft[:], scalar1=-1.0, scalar2=None, op0=ALU.mult, deps=[t_bf])
                    t_e = None
                    for c in range(4):
                        bk = PJ[c % 2]
                        pe.wait(bk.free)
                        tok = None
                        for kc in range(KC):
                            tok = pe.op("matmul", bk.t[0:8, :], lhsT=wf[:, kc, :], rhs=hT[:, kc, c * 512:(c + 1) * 512],
                                        start=(kc == 0), stop=(kc == KC - 1), sig=(kc == KC - 1), deps=[t_wf])
                        t_e = act.op("activation", out=ev[:, c * 512:(c + 1) * 512], in_=bk.t[0:8, :], func=AF.Exp, scale=-1.0,
                                     bias=nbf[:, 0:1], deps=[tok, t_nb])
                        bk.free = [t_e]
                    t_sp = act.op("activation", out=ev[:], in_=ev[:], func=AF.Ln, scale=1.0, bias=1.0, deps=[t_e])
                    t_o = dve.op("memset", ones8[:], 1.0)
                    t_G = dve.op("tensor_tensor_scan", out=G[:], data0=ones8[:], data1=ev[:], initial=0.0, op0=ALU.mult,
                                 op1=ALU.add, deps=[t_sp, t_o])
                    t_G8 = dve.op("tensor_scalar", out=G[:], in0=G[:], scalar1=8.0, scalar2=None, op0=ALU.mult, deps=[t_G])
                    t_hi = dve.op("tensor_copy", out=spl[0][:], in_=G[:], deps=[t_G8])
                    t_r1 = dve.op("tensor_tensor", out=ev[:], in0=G[:], in1=spl[0][:], op=ALU.subtract, deps=[t_hi])
                    t_mid = dve.op("tensor_copy", out=spl[1][:], in_=ev[:], deps=[t_r1])
                    t_r2 = dve.op("tensor_tensor", out=G[:], in0=ev[:], in1=spl[1][:], op=ALU.subtract, deps=[t_mid])
                    t_lo = dve.op("tensor_copy", out=spl[2][:], in_=G[:], deps=[t_r2])
                    if dbg:
                        d_G = dout("d_G", [8, S])
                        t_a = dve.op("tensor_tensor", out=ev[:], in0=spl[0][:], in1=spl[1][:], op=ALU.add, deps=[t_lo])
                        t_b = dve.op("tensor_tensor", out=ev[:], in0=ev[:], in1=spl[2][:], op=ALU.add, deps=[t_a])
                        gsl = new_slot()
                        tkg = sp.dma(d_G, ev[:], gsl, deps=[t_b])
                        barrier([tkg])

                wqb = sb(sf, "wqb", [128, KC, 512], BF16)
                wkb = sb(sf, "wkb", [128, KC, 512], BF16)
                wvb = sb(sf, "wvb", [128, KC, 512], BF16)
                wslots = [new_slot() for _ in range(3)]
                t_wv = pool.dma(wvb[:], w_in_v[:, :, C_VB:C_VB + 512], wslots[2])
                t_wq = pool.dma(wqb[:], w_in_v[:, :, C_QB:C_QB + 512], wslots[0])
                t_wk = pool.dma(wkb[:], w_in_v[:, :, C_KB:C_KB + 512], wslots[1])
                V_all = sb(sf, "V_all", [128, NT, 8, 66], BF16)
                Qh = [sb(sf, f"Qh{i}", [128, S], BF16) for i in range(4)]
                Kh = [sb(sf, f"Kh{i}", [128, S], BF16) for i in range(4)]
                Pb = [sb(sf, f"Pf{i}", [128, 1024], BF16) for i in range(4)]
                P_free = [[] for _ in range(4)]
                sring = [(dbl[k_], banks[2 * k_], banks[2 * k_ + 1]) for k_ in range(3)]
                ring_i = [0]

                def take_slot():
                    sl = sring[ring_i[0] % 3]
                    ring_i[0] += 1
                    return sl
                hq_slots = [new_slot() for _ in range(4)]
                t_vm = pool.op("memset", V_all[:].rearrange("p t h d -> p (t h d)"), 1.0)
                t_rd0 = pool.op("memset", rd[:], 0.0)
                state["res0"] = dict(rd=rd, bank=None, take=take_slot, bcs=bcs, rd_free=[t_rd0], bcs_free=[])
                h_init = []
                for i in range(4):
                    tq0 = pool.op("memset", Qh[i][:], 0.0)
                    tk0 = pool.op("memset", Kh[i][:], 0.0)
                    base = 64 if i % 2 == 0 else 0
                    pool.dma(Qh[i][base + 3:base + 6, :], c_rows[0:3, :], hq_slots[i], deps=[tq0])
                    h_init.append(pool.dma(Kh[i][base:base + 3, :], c_rows[3:6, :], hq_slots[i], deps=[tk0]))
                hfree = [[] for _ in range(4)]

                def prep_rows(h):
                    i = h % 4
                    base = 64 if i % 2 == 0 else 0
                    sp.wait([hfree[i], t_lo, h_init[i]])
                    tok = None
                    for r in range(3):
                        sp.dma(Qh[i][base + r:base + r + 1, :], spl[r][h:h + 1, :], hq_slots[i])
                        tok = sp.dma(Kh[i][base + 3 + r:base + 4 + r, :], spl[r][h:h + 1, :], hq_slots[i])
                    return tok

                for t in range(NT):
                    bk = PJ[t % 2]
                    pe.wait(bk.free)
                    tok = None
                    for kc in range(KC):
                        tok = pe.op("matmul", bk.t[:], lhsT=hT[:, kc, t * 128:(t + 1) * 128], rhs=wvb[:, kc, :],
                                    start=(kc == 0), stop=(kc == KC - 1), sig=(kc == KC - 1), deps=[t_wv])
                    src = bk.t[:].rearrange("p (h d) -> p h d", h=8)
                    t_ev = act.op("activation", out=V_all[:, t, :, 0:64], in_=src, func=AF.Copy, deps=[tok, t_vm])
                    bk.free = [t_ev]
                t_vall = [t_ev]
                v_done = []

                qk_ready = {}
                kaqa_tok = {}

                def proj_pair(hp):
                    iA = (2 * hp) % 4
                    iB = iA + 1
                    toks = []

                    def mk(dA, dB):
                        def evac(c, bk, tok):
                            cs = slice(c * 512, (c + 1) * 512)
                            dve.op("tensor_copy", out=dA[0:64, cs], in_=bk.t[0:64, :],
                                   deps=[tok, hfree[iA], hfree[iB], h_init[iA], h_init[iB]], sig=False)
                            t2 = dve.op("tensor_copy", out=dB[64:128, cs], in_=bk.t[64:128, :])
                            bk.free = [t2]
                            toks.append(t2)
                        return evac
                    for (ev_, wt_, tw_) in ((mk(Qh[iA], Qh[iB]), wqb, t_wq), (mk(Kh[iA], Kh[iB]), wkb, t_wk)):
                        for c in range(4):
                            _d, bA, bB = take_slot()
                            pe.wait([bA.free, bB.free])
                            tok = None
                            for kc in range(KC):
                                tok = pe.op("matmul", bA.t[:], lhsT=wt_[:, kc, hp * 128:(hp + 1) * 128], rhs=hT[:, kc, c * 512:(c + 1) * 512],
                                            start=(kc == 0), stop=(kc == KC - 1), sig=(kc == KC - 1), deps=([tw_] if kc == 0 else ()))
                            ev_(c, bA, tok)
                            bB.free = bA.free
                    qk_ready[hp] = toks
                    for hh in range(2):
                        kaqa_tok[2 * hp + hh] = prep_rows(2 * hp + hh)

                units = []
                for hp in range(4):
                    for hh in range(2):
                        for c in range(4):
                            nj = 4 * c + 4
                            for j0 in range(0, nj, 2):
                                its = [dict(j=j, last=(j == nj - 1), q0=max(j * 128, c * 512) - c * 512, diag=(j >= 4 * c))
                                       for j in (j0, j0 + 1)]
                                units.append(dict(hp=hp, hh=hh, h=2 * hp + hh, c=c, its=its, j0=j0))
                fO = [banks[6], banks[7]]
                chunk_ctr = [0]
                headB_toks = {}
                cur = {}

                def stage1(u, ui):
                    hp, hh, h, c = u["hp"], u["hh"], u["h"], u["c"]
                    if c == 0 and u["j0"] == 0 and hh == 0 and hp not in qk_ready:
                        proj_pair(hp)
                    if hh == 1 and c == 3 and u["j0"] == 0 and hp + 1 < 4:
                        proj_pair(hp + 1)
                    if hp == 3 and hh == 1 and c == 2 and u["j0"] == 0:
                        pre_issue(("D", 0, "q"), w_in_v[:, :, C_QA:C_QA + 256])
                        pre_issue(("D", 0, "k"), w_in_v[:, :, C_KA:C_KA + 256])
                        pre_issue(("D", 0, "v"), w_in_v[:, :, C_VA:C_VA + 256])
                    sd_, bA_, bB_ = take_slot()
                    u["dbl"] = sd_
                    u["bks"] = [bA_, bB_]
                    pe.wait([bA_.free, bB_.free])
                    tok = None
                    for k, it in enumerate(u["its"]):
                        q0, j = it["q0"], it["j"]
                        c0 = q0 if k == 0 else 512
                        it["c0"] = c0
                        qs = slice(c * 512 + q0, (c + 1) * 512)
                        ks = slice(j * 128, (j + 1) * 128)
                        tok = pe.op("matmul", u["dbl"][:, c0:c0 + 512 - q0], lhsT=Kh[h % 4][:, ks], rhs=Qh[h % 4][:, qs], start=True,
                                    stop=(not it["diag"]), sig=(k == 1 and not it["diag"]), deps=[qk_ready[hp], kaqa_tok[h]])
                        if it["diag"]:
                            tok = pe.op("matmul", u["dbl"][:, c0:c0 + 128], lhsT=ident[:], rhs=mask2[:, 128:256], start=False,
                                        stop=True, sig=(k == 1), deps=[const_tok])
                    u["s_tok"] = tok
                    hfree[h % 4] = [tok]
                    tick()

                def stage2(u, ui):
                    lo = u["its"][0]["q0"]
                    hi = 1024 - u["its"][1]["q0"]
                    u["p"] = Pb[ui % 4]
                    tok = act.op("activation", out=u["p"][:, lo:hi], in_=u["dbl"][:, lo:hi], func=AF.Exp, scale=0.125,
                                 deps=[u["s_tok"], P_free[ui % 4]])
                    u["bks"][0].free = [tok]
                    u["bks"][1].free = [tok]
                    u["p_tok"] = tok

                def stage3(u, ui):
                    hp, hh, h, c = u["hp"], u["hh"], u["h"], u["c"]
                    if u["j0"] == 0:
                        cur["obk"] = fO[chunk_ctr[0] % 2]
                        chunk_ctr[0] += 1
                        pe.wait(cur["obk"].free)
                    obk = cur["obk"]
                    tok = None
                    for k, it in enumerate(u["its"]):
                        q0, j, c0 = it["q0"], it["j"], it["c0"]
                        tok = pe.op("matmul", obk.t[0:65, q0:512], lhsT=V_all[:, j, h, 0:65], rhs=u["p"][:, c0:c0 + 512 - q0], start=(j == 0),
                                    stop=it["last"], sig=(k == 1), deps=[u["p_tok"], t_vall])
                    P_free[ui % 4] = [tok]
                    if u["its"][1]["last"]:
                        if hh == 0:
                            dst = OBT[0:64, hp, c * 512:(c + 1) * 512]
                        else:
                            dst = tmpB[0:64, c * 512:(c + 1) * 512]

                        def done(t_mul, obk=obk, hp=hp, hh=hh, c=c, h=h):
                            obk.free = [t_mul]
                            if hh == 1:
                                headB_toks.setdefault(hp, []).append(t_mul)
                                if c == 3:
                                    tk = sp.dma(OBT[64:128, hp, :], tmpB[0:64, :], mvslot, deps=[headB_toks[hp]])
                                    state["tmpB_free"] = [tk]
                                    v_done.append(tk)
                        extra = state["tmpB_free"] if hh == 1 else []
                        normalize(obk.t[0:64, :], obk.t[64:65, :], dst, [tok, extra], done)

                LAG = 2
                for ui, u in enumerate(units):
                    stage1(u, ui)
                    stage2(u, ui)
                    if ui >= LAG:
                        stage3(units[ui - LAG], ui - LAG)
                for ui in range(len(units) - LAG, len(units)):
                    stage3(units[ui], ui)
                flush()
                barrier([(mvslot.sem, mvslot.n)])

            if dbg:
                d_obt = dout("d_obt", [128, 4 * S])
                with contextlib.ExitStack() as sd:
                    obf = sb(sd, "dbg_obf", [128, 4 * S], F32)
                    t1 = dve.op("tensor_copy", out=obf[:], in_=OBT[:].rearrange("p k n -> p (k n)"))
                    dsl = new_slot()
                    sp.dma(d_obt, obf[:], dsl, deps=[t1])
                    sp.wait([(dsl.sem, dsl.n)])
                    barrier()
            with contextlib.ExitStack() as sdl:
                bcs = sb(sdl, "bcs_d", [64, 512], F32)
                tmpB = sb(sdl, "tmpB_d", [64, S], BF16)
                bcs2 = sb(sdl, "bcs_d2", [64, 512], F32)
                state.update(tmpB_free=[])
                cosf = sb(sdl, "cosf", [128, S], F32)
                sinf = sb(sdl, "sinf", [128, S], F32)
                cs_slot = new_slot()
                ACC = sb(sdl, "ACC", [65, 4, S], F32)
                QTd = [sb(sdl, f"QTd{i}", [128, S], BF16) for i in range(2)]
                Kd = [[sb(sdl, f"Kd{i}_{j}", [128, S], BF16) for j in range(2)] for i in range(2)]
                KTp = [sb(sdl, f"KTp{i}", [128, S], BF16) for i in range(2)]
                kd_slots = [[new_slot() for _ in range(2)] for _ in range(2)]
                ktp_free = [[], []]

                Vg = sb(sdl, "Vg", [128, 16, 4, 66], BF16)
                qbt = [sb(sdl, f"qbt{i}", [128, 512], BF16) for i in range(2)]
                t2b = [sb(sdl, f"t2b{i}", [128, 512], F32) for i in range(2)]
                Pd = [sb(sdl, f"Pd{i}", [128, 512], BF16) for i in range(4)]
                Pd_free = [[] for _ in range(4)]
                dil_ms = {}
                Bbanks = [banks[5], banks[2]]
                rp = dict(i=0, qbt_free=[[], []], t1_free=[[], []], t2_free=[[], []])
                dil_state = dict(qk_free=[], v_free=[], octr=0, sidx=0)

                def rope_proj(dsts, wtile, col0, deps_w, dst_free):
                    done_toks = []

                    def rest(c, bk, tokA, t_qb, i):
                        Bb = Bbanks[i % 2]
                        pe.wait(Bb.free)
                        t_B = pe.op("matmul", Bb.t[:], lhsT=perm[:], rhs=qbt[i % 2][:], start=True, stop=True,
                                    deps=[t_qb, const_tok])
                        rp["qbt_free"][i % 2] = [t_B]
                        cs = slice(c * 512, (c + 1) * 512)
                        t_1 = dve.op("tensor_tensor", out=bk.t[:], in0=bk.t[:], in1=cosf[:, cs], op=ALU.mult,
                                     deps=[tokA, t_qb, t_cs])
                        t_2 = dve.op("tensor_tensor", out=t2b[i % 2][:], in0=Bb.t[:], in1=sinf[:, cs], op=ALU.mult,
                                     deps=[t_B, rp["t2_free"][i % 2]])
                        Bb.free = [t_2]
                        t_3 = dve.op("tensor_tensor", out=dsts[:, cs], in0=bk.t[:], in1=t2b[i % 2][:], op=ALU.add,
                                     deps=[t_1, t_2, dst_free])
                        bk.free = [t_3]
                        rp["t2_free"][i % 2] = [t_3]
                        done_toks.append(t_3)

                    prev = None
                    for c in range(4):
                        bk = [banks[6], banks[7], banks[0], banks[1]][rp["i"] % 4]
                        pe.wait(bk.free)
                        tok = None
                        for kc in range(KC):
                            tok = pe.op("matmul", bk.t[:], lhsT=wtile[:, kc, col0:col0 + 128], rhs=hT[:, kc, c * 512:(c + 1) * 512],
                                        start=(kc == 0), stop=(kc == KC - 1), sig=(kc == KC - 1), deps=(deps_w if kc == 0 else ()))
                        i = rp["i"]
                        rp["i"] += 1
                        t_qb = act.op("activation", out=qbt[i % 2][:], in_=bk.t[:], func=AF.Copy, deps=[tok, rp["qbt_free"][i % 2]])
                        if prev is not None:
                            rest(*prev)
                        prev = (c, bk, tok, t_qb, i)
                    rest(*prev)
                    return done_toks, tok

                for g in range(int(os.environ.get('DBG_NG', '3'))):
                    d = DIL[g]
                    nb = 16 // d
                    wi = g % 2
                    wq_t, t_wq, rel_q = get_full(("D", g, "q"), w_in_v[:, :, C_QA + g * 256:C_QA + (g + 1) * 256])
                    if g == 0:
                        pool.dma(cosf[:], c_cosf, cs_slot)
                        t_cs = pool.dma(sinf[:], c_sinf, cs_slot)
                    wk_t, t_wk, rel_k = get_full(("D", g, "k"), w_in_v[:, :, C_KA + g * 256:C_KA + (g + 1) * 256])
                    wv_t, t_wv, rel_v = get_full(("D", g, "v"), w_in_v[:, :, C_VA + g * 256:C_VA + (g + 1) * 256])
                    if g == 0:
                        t_kd0 = [pool.op("memset", Kd[i][j][:], 0.0) for i in range(2) for j in range(2)]
                        t_vgm = pool.op("memset", Vg[:].rearrange("p t h d -> p (t h d)"), 1.0)
                    qk_ready = []
                    for pr in range(2):
                        dq, lastq = rope_proj(QTd[pr], wq_t, pr * 128, [t_wq], dil_state["qk_free"])
                        dk, lastk = rope_proj(KTp[pr], wk_t, pr * 128, [t_wk], ktp_free[pr])
                        tkA = sp.dma(Kd[pr][0][0:64, :], KTp[pr][0:64, :], kd_slots[pr][0], deps=[dk, dil_state["qk_free"], t_kd0])
                        tkB = sp.dma(Kd[pr][1][64:128, :], KTp[pr][64:128, :], kd_slots[pr][1], deps=[dk, dil_state["qk_free"], t_kd0])
                        ktp_free[pr] = [tkA, tkB]
                        qk_ready += dq + [tkA, tkB]
                    rel_q([lastq])
                    rel_k([lastk])
                    t_ev = None
                    for b in range(16):
                        r, n = divmod(b, nb)
                        bk = PJ[b % 2]
                        pe.wait(bk.free)
                        tok = None
                        for kc in range(KC):
                            lv = hT[:, kc, :].rearrange("p (i r) -> p r i", r=d)[:, r, 128 * n:128 * (n + 1)]
                            tok = pe.op("matmul", bk.t[:, 0:256], lhsT=lv, rhs=wv_t[:, kc, :], start=(kc == 0), stop=(kc == KC - 1),
                                        sig=(kc == KC - 1), deps=[t_wv])
                        src = bk.t[:, 0:256].rearrange("p (h d) -> p h d", h=4)
                        if b % 2 == 0:
                            t_ev = act.op("activation", out=Vg[:, b, :, 0:64], in_=src, func=AF.Copy, deps=[tok, t_vgm, dil_state["v_free"]])
                        else:
                            t_ev = dve.op("tensor_copy", out=Vg[:, b, :, 0:64], in_=src, deps=[tok, t_vgm, dil_state["v_free"]])
                        bk.free = [t_ev]
                    rel_v([tok])
                    if g == 2:
                        pool.dma(WA[:], w_a_v, ab_slot)
                        ab_tok = [pool.dma(WB[:], w_b_v, ab_slot)]
                        for ocg_ in range(2):
                            pre_issue(("C", 0, ocg_, "a"), w_in_v[:, :, C_GA + ocg_ * 256:C_GA + (ocg_ + 1) * 256])
                            pre_issue(("C", 0, ocg_, "b"), w_in_v[:, :, C_GB + ocg_ * 256:C_GB + (ocg_ + 1) * 256])
                    v_ready = [t_ev]
                    if True:
                        v_ready = []
                        v_ready = [(act.sem, act.n), (dve.sem, dve.n)]
                    ditems = [dict(hh=hh, bs=(b0, b0 + 1)) for hh in range(4) for b0 in range(0, 16, 2)]
                    if os.environ.get('DBG_NOATT'):
                        ditems = []
                        dil_state['acc_tok'] = []

                    def dstage1(u):
                        hh = u["hh"]
                        pr = hh // 2
                        Qv = QTd[pr][:, :].rearrange("p (i r) -> p r i", r=d)
                        Kv = Kd[pr][hh % 2][:, :].rearrange("p (i r) -> p r i", r=d)
                        i = dil_state["sidx"]
                        dil_state["sidx"] += 1
                        u["i"] = i
                        sbk = S_banks[i % 3]
                        u["sbk"] = sbk
                        pe.wait(sbk.free)
                        pe.wait(qk_ready)
                        pe.wait(const_tok)
                        tok = None
                        for k, b in enumerate(u["bs"]):
                            r, n = divmod(b, nb)
                            off = 256 * k
                            qsl = Qv[:, r, 128 * n:128 * (n + 1)]
                            if n >= 1:
                                pe.op("matmul", sbk.t[:, off:off + 128], lhsT=Kv[:, r, 128 * (n - 1):128 * n], rhs=qsl, start=True, stop=False,
                                      sig=False)
                            pe.op("matmul", sbk.t[:, off + 128:off + 256], lhsT=Kv[:, r, 128 * n:128 * (n + 1)], rhs=qsl, start=(n == 0),
                                  stop=False, sig=False)
                            if n >= 1:
                                tok = pe.op("matmul", sbk.t[:, off:off + 256], lhsT=ident[:], rhs=mask2[:, 0:256], start=False, stop=True,
                                            sig=(k == 1))
                            else:
                                tok = pe.op("matmul", sbk.t[:, off + 128:off + 256], lhsT=ident[:], rhs=mask2[:, 128:256], start=False,
                                            stop=True, sig=(k == 1))
                        u["s_tok"] = tok
                        dil_state["qk_free"] = [tok]

                    def dstage2(u):
                        r0, n0 = divmod(u["bs"][0], nb)
                        lo = 0 if n0 >= 1 else 128
                        i = u["i"]
                        tok = act.op("activation", out=Pd[i % 4][:, lo:512], in_=u["sbk"].t[:, lo:512], func=AF.Exp, scale=0.125,
                                     deps=[u["s_tok"], Pd_free[i % 4]])
                        u["sbk"].free = [tok]
                        u["p_tok"] = tok

                    def dstage3(u):
                        for k_, b_ in enumerate(u["bs"]):
                            dstage3_item(u, k_, b_)

                    def dstage3_item(it, k_, b):
                        hh = it["hh"]
                        poff = 256 * k_
                        r, n = divmod(b, nb)
                        i = it["i"]
                        col = (b % 4) * 128
                        if b % 4 == 0:
                            dil_state["obk"] = O_banks[dil_state["octr"] % 2]
                            dil_state["octr"] += 1
                            pe.wait(dil_state["obk"].free)
                        obk = dil_state["obk"]
                        pe.wait([it["p_tok"], v_ready])
                        if n >= 1:
                            pe.op("matmul", obk.t[0:65, col:col + 128], lhsT=Vg[:, b - 1, hh, 0:65], rhs=Pd[i % 4][:, poff:poff + 128], start=True,
                                  stop=False, sig=False)
                        tok = pe.op("matmul", obk.t[0:65, col:col + 128], lhsT=Vg[:, b, hh, 0:65], rhs=Pd[i % 4][:, poff + 128:poff + 256],
                                    start=(n == 0), stop=True)
                        Pd_free[i % 4] = [tok]
                        dil_state["v_free"] = [tok]
                        if b % 4 == 3:
                            q = b // 4
                            if d == 1:
                                dstv = ACC[0:65, hh, 512 * q:512 * (q + 1)]
                                src = obk.t[0:65, :]
                            elif d == 4:
                                dstv = ACC[0:65, hh, :].rearrange("p (i r) -> p r i", r=4)[:, q, :]
                                src = obk.t[0:65, :]
                            else:
                                dstv = ACC[0:65, hh, :].rearrange("p (f r) -> p r f", r=16)[:, 4 * q:4 * q + 4, :]
                                src = obk.t[0:65, :].rearrange("p (r f) -> p r f", r=4)
                            if g == 0:
                                if q % 2 == 0:
                                    t_a = act.op("activation", out=dstv, in_=src, func=AF.Copy, deps=[tok])
                                else:
                                    t_a = dve.op("tensor_copy", out=dstv, in_=src, deps=[tok])
                            else:
                                t_a = dve.op("tensor_tensor", out=dstv, in0=src, in1=dstv, op=ALU.add, deps=[tok, it.get("acc_dep", [])])
                            obk.free = [t_a]
                            dil_state["acc_tok"] = [(act.sem, act.n), (dve.sem, dve.n)]

                    LAGD = 2
                    for i, it in enumerate(ditems):
                        if g > 0:
                            it["acc_dep"] = dil_state.get("acc_prev", [])
                        dstage1(it)
                        dstage2(it)
                        if i >= LAGD:
                            dstage3(ditems[i - LAGD])
                    for i in range(max(0, len(ditems) - LAGD), len(ditems)):
                        dstage3(ditems[i])
                    dil_state["acc_prev"] = dil_state.get("acc_tok", [])

                acc_done = dil_state["acc_tok"]
                zt = [t2b[0], t2b[1]]
                t_z = [dve.op("memset", z_[:], 0.0, deps=[acc_done]) for z_ in zt]
                bshare = dict(bcs_free=[])
                ress = []
                for k in range(2):
                    ress.append(dict(rd=zt[k], bank=banks[k], bcs=(bcs if k % 2 == 0 else bcs2), rd_free=[t_z[k]], bcs_free=[]))
                hb_toks = {}
                nctr = 0
                for hh in range(4):
                    for c in range(4):
                        cs = slice(c * 512, (c + 1) * 512)
                        if hh % 2 == 0:
                            dst = OAT[0:64, hh // 2, cs]
                            extra = []
                        else:
                            dst = tmpB[0:64, cs]
                            extra = state["tmpB_free"]
                        res = ress[nctr % 2]
                        nctr += 1

                        def done(t_mul, hh=hh, c=c):
                            if hh % 2 == 1:
                                hb_toks.setdefault(hh, []).append(t_mul)
                                if c == 3:
                                    tk = sp.dma(OAT[64:128, hh // 2, :], tmpB[0:64, :], mvslot, deps=[hb_toks[hh]])
                                    state["tmpB_free"] = [tk]
                        normalize(ACC[0:64, hh, cs], ACC[64:65, hh, cs], dst, [acc_done, extra], done, res=res)
                        tick()
                flush()
                barrier([(mvslot.sem, mvslot.n)])

            if dbg:
                d_oat = dout("d_oat", [128, 2 * S])
                with contextlib.ExitStack() as sd:
                    oaf = sb(sd, "dbg_oaf", [128, 2 * S], F32)
                    t1 = dve.op("tensor_copy", out=oaf[:], in_=OAT[:].rearrange("p k n -> p (k n)"))
                    dsl = new_slot()
                    sp.dma(d_oat, oaf[:], dsl, deps=[t1])
                    sp.wait([(dsl.sem, dsl.n)])
                    barrier()

            with contextlib.ExitStack() as s2:
                gpost = sb(s2, "gpost", [128, D], F32)
                gfpre = sb(s2, "gfpre", [128, D], F32)
                gfpost = sb(s2, "gfpost", [128, D], F32)
                x1h = sb(s2, "x1h", [128, 8, D], F32)
                ssq = sb(s2, "ssq", [128, 64], F32)
                sml = sb(s2, "sml", [128, 64], F32)
                rms_ctr = [0]
                g2slot = new_slot()
                sp.dma(gpost[:], g_post.partition_broadcast(128), g2slot)
                sp.dma(gfpre[:], g_fpre.partition_broadcast(128), g2slot)
                t_gains = sp.dma(gfpost[:], g_fpost.partition_broadcast(128), g2slot)
                out_slots = [new_slot() for _ in range(4)]
                out_toks = [None, None, None, None]
                xt_slots = [new_slot() for _ in range(2)]
                wo_slot = new_slot()


                def soft_barrier(extra=()):
                    ta = (act.sem, act.n)
                    td = (dve.sem, dve.n)
                    tp = (pe.sem, pe.n)
                    ex = list(extra)
                    act.wait([td, tp, ex])
                    dve.wait([ta, tp, ex])
                    pool.wait([ta, td, tp, ex])
                    sp.wait([ta, td, tp, ex])

                def rms_rstd(sq_bank_aps, junk_t, deps, col):
                    n = len(sq_bank_aps)
                    col = 2 * (rms_ctr[0] % 16) + col
                    rms_ctr[0] += 1
                    t = None
                    for k, ap in enumerate(sq_bank_aps):
                        t = act.op("activation", out=junk_t[:, 0:ap.shape[-1]], in_=ap, func=AF.Square, accum_out=ssq[:, 2 * col + k:2 * col + k + 1],
                                   deps=[deps])
                    if n == 2:
                        t = dve.op("tensor_tensor", out=ssq[:, 2 * col:2 * col + 1], in0=ssq[:, 2 * col:2 * col + 1],
                                   in1=ssq[:, 2 * col + 1:2 * col + 2], op=ALU.add, deps=[t])
                    t = act.op("activation", out=sml[:, 2 * col:2 * col + 1], in_=ssq[:, 2 * col:2 * col + 1], func=AF.Ln, scale=1.0 / D,
                               bias=EPS, deps=[t])
                    t = act.op("activation", out=sml[:, 2 * col + 1:2 * col + 2], in_=sml[:, 2 * col:2 * col + 1], func=AF.Exp, scale=-0.5,
                               deps=[t])
                    return t, sml[:, 2 * col + 1:2 * col + 2]

                h2T_tok = {}
                e_pre_all = {}
                for p in range(2):
                    with contextlib.ExitStack() as scd:
                        mergedTh = sb(scd, "mergedTh", [128, KC, 1024], BF16)
                        Wout = sb(scd, "Wout", [128, KC, D], BF16)
                        sa_t = [sb(scd, f"sa{i}", [128, 512], F32) for i in range(2)]
                        sb_t = [sb(scd, f"sbt{i}", [128, 512], F32) for i in range(2)]
                        xt = [sb(scd, f"xt{i}", [128, D], F32) for i in range(2)]
                        n1 = sb(scd, "n1", [128, D], F32)
                        h2b = [sb(scd, f"h2b{i}", [128, D], BF16) for i in range(4)]
                        junk = sb(scd, "junkD", [128, D], BF16)
                        cw_tok = None
                        sa_free = [[], []]
                        kctr = 0
                        m_toks = []
                        for ocg in range(4):
                            wga_t, t_ga, rel_ga = get_full(("C", p, ocg, "a"), w_in_v[:, :, C_GA + ocg * 256:C_GA + (ocg + 1) * 256])
                            wgb_t, t_gb, rel_gb = get_full(("C", p, ocg, "b"), w_in_v[:, :, C_GB + ocg * 256:C_GB + (ocg + 1) * 256])
                            if ocg == 1:
                                cw_tok = [pool.dma(Wout[:], w_o_v, wo_slot)]
                            for ocl in range(2):
                                oc = 2 * ocg + ocl
                                for cl in range(2):
                                    c = 2 * p + cl
                                    cs = slice(c * 512, (c + 1) * 512)
                                    base = 4 * (kctr % 2)
                                    ri = kctr % 2
                                    kctr += 1
                                    bGA, bGB, bPA, bPB = banks[base], banks[base + 1], banks[base + 2], banks[base + 3]
                                    tGA = tGB = tPA = tPB = None
                                    pe.wait(bGA.free)
                                    for kc in range(KC):
                                        tGA = pe.op("matmul", bGA.t[:], lhsT=wga_t[:, kc, ocl * 128:(ocl + 1) * 128], rhs=hT[:, kc, cs],
                                                    start=(kc == 0), stop=(kc == KC - 1), sig=(kc == KC - 1), deps=[t_ga])
                                    pe.wait(bGB.free)
                                    for kc in range(KC):
                                        tGB = pe.op("matmul", bGB.t[:], lhsT=wgb_t[:, kc, ocl * 128:(ocl + 1) * 128], rhs=hT[:, kc, cs],
                                                    start=(kc == 0), stop=(kc == KC - 1), sig=(kc == KC - 1), deps=[t_gb])
                                    pe.wait(bPA.free)
                                    for k2 in range(2):
                                        tPA = pe.op("matmul", bPA.t[:], lhsT=WA[:, k2, oc * 128:(oc + 1) * 128], rhs=OAT[:, k2, cs],
                                                    start=(k2 == 0), stop=(k2 == 1), sig=(k2 == 1), deps=[ab_tok])
                                    pe.wait(bPB.free)
                                    for k4 in range(4):
                                        tPB = pe.op("matmul", bPB.t[:], lhsT=WB[:, k4, oc * 128:(oc + 1) * 128], rhs=OBT[:, k4, cs],
                                                    start=(k4 == 0), stop=(k4 == 3), sig=(k4 == 3))
                                    rel_ga([tGA])
                                    rel_gb([tGB])
                                    t_sa = act.op("activation", out=sa_t[ri][:], in_=bGA.t[:], func=AF.Sigmoid, deps=[tGA, sa_free[ri]])
                                    t_sb = act.op("activation", out=sb_t[ri][:], in_=bGB.t[:], func=AF.Sigmoid, deps=[tGB])
                                    t_m1 = dve.op("tensor_tensor", out=sa_t[ri][:], in0=bPA.t[:], in1=sa_t[ri][:], op=ALU.mult, deps=[tPA, t_sa])
                                    t_m2 = dve.op("tensor_tensor", out=sb_t[ri][:], in0=bPB.t[:], in1=sb_t[ri][:], op=ALU.mult, deps=[tPB, t_sb])
                                    bGA.free = [t_m2]
                                    bGB.free = [t_m2]
                                    bPA.free = [t_m2]
                                    bPB.free = [t_m2]
                                    t_mg = dve.op("tensor_tensor", out=mergedTh[:, oc, cl * 512:(cl + 1) * 512], in0=sa_t[ri][:], in1=sb_t[ri][:],
                                                  op=ALU.add, deps=[t_m1, t_m2])
                                    sa_free[ri] = [t_mg]
                                    m_toks.append(t_mg)
                        if dbg and p == 0:
                            d_merged = dout("d_mergedh", [128, KC * 1024])
                            with contextlib.ExitStack() as sd:
                                mf = sb(sd, "dbg_mf", [128, 1024], F32)
                                dsl = new_slot()
                                tk = None
                                for kc in range(KC):
                                    t1 = dve.op("tensor_copy", out=mf[:], in_=mergedTh[:, kc, :], deps=[m_toks, tk])
                                    tk = sp.dma(d_merged[:, kc * 1024:(kc + 1) * 1024], mf[:], dsl, deps=[t1])
                                barrier([tk])
                        merged_ready = [(dve.sem, dve.n)]
                        for jg_ in range(2):
                            pre_issue(("E", p, jg_, "g"), w_g_v[:, :, jg_ * 256:(jg_ + 1) * 256])
                            pre_issue(("E", p, jg_, "u"), w_u_v[:, :, jg_ * 256:(jg_ + 1) * 256])
                        xt_free = [[], []]
                        h2b_free = [[], [], [], []]
                        n1_free = []
                        infos = {}

                        def stage_X(tl):
                            t = 8 * p + tl
                            i = tl % 2
                            i4 = tl % 4
                            t_x = sp.dma(xt[i][:], x[t * 128:(t + 1) * 128, :], xt_slots[i], deps=[xt_free[i]])
                            m3i = tl % 3
                            bM = [banks[2 * m3i], banks[2 * m3i + 1]]
                            pe.wait([bM[0].free, bM[1].free, merged_ready, cw_tok])
                            tM = [None, None]
                            for hc in range(2):
                                for kc in range(KC):
                                    tM[hc] = pe.op("matmul", bM[hc].t[:], lhsT=mergedTh[:, kc, tl * 128:(tl + 1) * 128],
                                                   rhs=Wout[:, kc, hc * 512:(hc + 1) * 512], start=(kc == 0), stop=(kc == KC - 1),
                                                   sig=(kc == KC - 1))
                            t_r, rs_ap = rms_rstd([bM[0].t[:], bM[1].t[:]], junk, [tM[0], tM[1]], 0)
                            t_n = None
                            for hc in range(2):
                                t_n = dve.op("scalar_tensor_tensor", out=n1[:, hc * 512:(hc + 1) * 512], in0=bM[hc].t[:], scalar=rs_ap,
                                             in1=gpost[:, hc * 512:(hc + 1) * 512], op0=ALU.mult, op1=ALU.mult,
                                             deps=[t_r, t_gains, state.get("n1_free", [])])
                                bM[hc].free = [t_n]
                            t_x1 = dve.op("tensor_tensor", out=x1h[:, tl, :], in0=xt[i][:], in1=n1[:], op=ALU.add, deps=[t_x, t_n])
                            xt_free[i] = [t_x1]
                            state["n1_free"] = [t_x1]
                            infos[tl] = dict(t=t, i=i, t_x1=t_x1)

                        def stage_Y(tl):
                            inf = infos[tl]
                            i = tl % 4
                            t_r2, rs2_ap = rms_rstd([x1h[:, tl, :]], junk, [inf["t_x1"]], 1)
                            inf["t_h2"] = dve.op("scalar_tensor_tensor", out=h2b[i][:], in0=x1h[:, tl, :], scalar=rs2_ap, in1=gfpre[:],
                                                 op0=ALU.mult, op1=ALU.mult, deps=[t_r2, h2b_free[i]])

                        def emit_T(tl):
                            inf = infos[tl]
                            t, i = inf["t"], tl % 4
                            bk = banks[6 + tl % 2]
                            t_tr = transposes_to(bk, h2b[i], [inf["t_h2"]])
                            h2b_free[i] = [t_tr]
                            if tl % 2 == 0:
                                t_cp = act.op("activation", out=hT[:, :, t * 128:(t + 1) * 128],
                                              in_=bank_bf16(bk).rearrange("p (k n) -> p k n", k=KC), func=AF.Copy, deps=[t_tr])
                            else:
                                t_cp = dve.op("tensor_copy", out=hT[:, :, t * 128:(t + 1) * 128],
                                              in_=bank_bf16(bk).rearrange("p (k n) -> p k n", k=KC), deps=[t_tr])
                            bk.free = [t_cp]
                            h2T_tok[t] = t_cp

                        e_pre = {}
                        e_w0 = {}

                        def pre_E_unit(jl, kctr_):
                            if not e_w0:
                                e_w0["g"] = get_full(("E", p, 0, "g"), w_g_v[:, :, 0:256])
                                e_w0["u"] = get_full(("E", p, 0, "u"), w_u_v[:, :, 0:256])
                            wg_t_, t_wg_, _ = e_w0["g"]
                            wu_t_, t_wu_, _ = e_w0["u"]
                            c_ = 2 * p
                            cs_ = slice(c_ * 512, (c_ + 1) * 512)
                            bG_, bU_ = banks[2 * (kctr_ % 4)], banks[2 * (kctr_ % 4) + 1]
                            pe.wait([bG_.free, bU_.free])
                            tG_ = tU_ = None
                            for kc in range(KC):
                                tG_ = pe.op("matmul", bG_.t[:], lhsT=wg_t_[:, kc, jl * 128:(jl + 1) * 128], rhs=hT[:, kc, cs_],
                                            start=(kc == 0), stop=(kc == KC - 1), sig=(kc == KC - 1),
                                            deps=[t_wg_, [h2T_tok[4 * c_ + q_] for q_ in range(4)]])
                            for kc in range(KC):
                                tU_ = pe.op("matmul", bU_.t[:], lhsT=wu_t_[:, kc, jl * 128:(jl + 1) * 128], rhs=hT[:, kc, cs_],
                                            start=(kc == 0), stop=(kc == KC - 1), sig=(kc == KC - 1), deps=[t_wu_])
                            e_pre[(jl, 0)] = (tG_, tU_)

                        for step in range(8 + 3):
                            if step < 8:
                                stage_X(step)
                            if 0 <= step - 1 < 8:
                                stage_Y(step - 1)
                            if step == 9:
                                pre_E_unit(0, 0)
                                pre_E_unit(1, 2)
                            if 0 <= step - 3 < 8:
                                emit_T(step - 3)
                        e_pre_all[p] = (e_pre, e_w0)
                        h2T_ready = [(act.sem, act.n), (dve.sem, dve.n)]
                        soft_barrier()
                    if dbg and p == 0:
                        d_x1 = dout("d_x1h", [128, 8 * D])
                        dsl = new_slot()
                        tk = sp.dma(d_x1, x1h[:].rearrange("p t n -> p (t n)"), dsl)
                        barrier([tk])
                    with contextlib.ExitStack() as sef:
                        uTh = sb(sef, "uTh", [128, NJ, 1024], BF16)
                        sg_t = [sb(sef, f"sg{i}", [128, 512], F32) for i in range(2)]
                        ot = [sb(sef, f"ot{i}", [128, D], F32) for i in range(4)]
                        junk = sb(sef, "junkF", [128, D], BF16)
                        sg_free = [[], []]
                        kctr = 0
                        for jg in range(11):
                            e_pre, e_w0 = e_pre_all.get(p, ({}, {}))
                            if jg == 0 and e_w0:
                                wg_t, t_wg, rel_g = e_w0["g"]
                                wu_t, t_wu, rel_u = e_w0["u"]
                            else:
                                wg_t, t_wg, rel_g = get_full(("E", p, jg, "g"), w_g_v[:, :, jg * 256:(jg + 1) * 256])
                                wu_t, t_wu, rel_u = get_full(("E", p, jg, "u"), w_u_v[:, :, jg * 256:(jg + 1) * 256])
                            for jl in range(2):
                                j = 2 * jg + jl
                                for cl in range(2):
                                    c = 2 * p + cl
                                    cs = slice(c * 512, (c + 1) * 512)
                                    base = 2 * (kctr % 4)
                                    ri = kctr % 2
                                    kctr += 1
                                    bG, bU = banks[base], banks[base + 1]
                                    if jg == 0 and (jl, cl) in e_pre:
                                        tG, tU = e_pre[(jl, cl)]
                                    else:
                                        pe.wait([bG.free, bU.free])
                                        tG = tU = None
                                        for kc in range(KC):
                                            tG = pe.op("matmul", bG.t[:], lhsT=wg_t[:, kc, jl * 128:(jl + 1) * 128], rhs=hT[:, kc, cs],
                                                       start=(kc == 0), stop=(kc == KC - 1), sig=(kc == KC - 1),
                                                       deps=[t_wg, [h2T_tok[4 * c + q_] for q_ in range(4)]])
                                        for kc in range(KC):
                                            tU = pe.op("matmul", bU.t[:], lhsT=wu_t[:, kc, jl * 128:(jl + 1) * 128], rhs=hT[:, kc, cs],
                                                       start=(kc == 0), stop=(kc == KC - 1), sig=(kc == KC - 1), deps=[t_wu])
                                    rel_g([tG])
                                    rel_u([tU])
                                    t_sg = act.op("activation", out=sg_t[ri][:], in_=bG.t[:], func=AF.Silu, deps=[tG, sg_free[ri]])
                                    t_u = dve.op("tensor_tensor", out=uTh[:, j, cl * 512:(cl + 1) * 512], in0=bU.t[:], in1=sg_t[ri][:],
                                                 op=ALU.mult, deps=[tU, t_sg])
                                    bG.free = [t_u]
                                    bU.free = [t_u]
                                    sg_free[ri] = [t_u]
                        u_ready = [(dve.sem, dve.n)]
                        for cl in range(2):
                            pe.wait(u_ready)
                            tF = [None] * 8
                            NPRE = 6
                            NPOST = 4
                            order = [(kc, t4) for t4 in range(4) for kc in range(NPRE)] + \
                                    [(kc, t4) for kc in range(NPRE, NJ - NPOST) for t4 in range(4)] + \
                                    [(kc, t4) for t4 in range(4) for kc in range(NJ - NPOST, NJ)]
                            wd_loaded = {}
                            for kc_ in range(NPRE):
                                wd_loaded[kc_] = ring_half(w_d[kc_ * 128:(kc_ + 1) * 128, :])
                            for kc, t4 in order:
                                if kc not in wd_loaded:
                                    wd_loaded[kc] = ring_half(w_d[kc * 128:(kc + 1) * 128, :])
                                wd_t, t_w, rel_d = wd_loaded[kc]
                                tl = 4 * cl + t4
                                for hc in range(2):
                                    bk_ = banks[2 * t4 + hc]
                                    if kc == 0:
                                        pe.wait(bk_.free)
                                    tF[2 * t4 + hc] = pe.op("matmul", bk_.t[:], lhsT=uTh[:, kc, tl * 128:(tl + 1) * 128],
                                                            rhs=wd_t[:, hc * 512:(hc + 1) * 512], start=(kc == 0), stop=(kc == NJ - 1),
                                                            sig=(kc == NJ - 1 or (t4 == 3 and hc == 1)), deps=[t_w])
                                if t4 == 3:
                                    rel_d([tF[7]])
                            t_ns = []
                            for t4 in range(4):
                                b0, b1 = banks[2 * t4], banks[2 * t4 + 1]
                                t_r, rs_ap = rms_rstd([b0.t[:], b1.t[:]], junk, [tF[2 * t4], tF[2 * t4 + 1]], 0)
                                t_n = None
                                for hc, bb in enumerate((b0, b1)):
                                    t_n = dve.op("scalar_tensor_tensor", out=ot[t4][:, hc * 512:(hc + 1) * 512], in0=bb.t[:], scalar=rs_ap,
                                                 in1=gfpost[:, hc * 512:(hc + 1) * 512], op0=ALU.mult, op1=ALU.mult,
                                                 deps=[t_r, t_gains, out_toks[t4]])
                                    bb.free = [t_n]
                                t_ns.append(t_n)
                            for t4 in range(4):
                                tl = 4 * cl + t4
                                t = 8 * p + tl
                                t_o = dve.op("tensor_tensor", out=ot[t4][:], in0=ot[t4][:], in1=x1h[:, tl, :], op=ALU.add, deps=[t_ns[t4]])
                                out_toks[t4] = sp.dma(out[t * 128:(t + 1) * 128, :], ot[t4][:], out_slots[t4], deps=[t_o])
                        if p == 0:
                            for ocg_ in range(2):
                                pre_issue(("C", 1, ocg_, "a"), w_in_v[:, :, C_GA + ocg_ * 256:C_GA + (ocg_ + 1) * 256])
                                pre_issue(("C", 1, ocg_, "b"), w_in_v[:, :, C_GB + ocg_ * 256:C_GB + (ocg_ + 1) * 256])
                        soft_barrier([tk_ for tk_ in out_toks if tk_ is not None])

        barrier()
    return nc, dbg_outs


def make_in_maps(inputs):
    c = host_consts()
    x = np.ascontiguousarray(inputs["x"], dtype=np.float32)
    shared = dict(
        w_in=np.ascontiguousarray(inputs["w_in"][0]),
        w_a=np.ascontiguousarray(inputs["w_proj_a"][0]),
        w_b=np.ascontiguousarray(inputs["w_proj_b"][0]),
        w_o=np.ascontiguousarray(inputs["w_out"][0]),
        b_f=np.ascontiguousarray(inputs["b_forget"][0].reshape(8, 1)),
        w_g=np.ascontiguousarray(inputs["w_ffn_gate"][0]),
        w_u=np.ascontiguousarray(inputs["w_ffn_up"][0]),
        w_d=np.ascontiguousarray(inputs["w_ffn_down"][0]),
        g_pre=np.ascontiguousarray(inputs["norm_mix_pre"][0].reshape(1, D)),
        g_post=np.ascontiguousarray(inputs["norm_mix_post"][0].reshape(1, D)),
        g_fpre=np.ascontiguousarray(inputs["norm_ffn_pre"][0].reshape(1, D)),
        g_fpost=np.ascontiguousarray(inputs["norm_ffn_post"][0].reshape(1, D)),
        **c,
    )
    return [dict(x=x[b], **shared) for b in range(x.shape[0])]


def kernel(**inputs):
    nc, _ = build(False)
    in_maps = make_in_maps(inputs)
    res = run_bass_kernel_spmd(nc, in_maps, core_ids=list(range(8)))
    return np.stack([np.asarray(r["out"], dtype=np.float32) for r in res.results], axis=0)
```
